# Optimizing a Trainium2 kernel written in Bass

```python
import math
import jax, jax.numpy as jnp
from jax import lax
import numpy as np

D_MODEL = 1024
BATCH = 4
SEQ = 4096
DEPTH = 2
DEC_BATCH = 32
DEC_SEQ = 16
PAST_LEN = 1024

CHUNK = 64
MIX_WIDTH = D_MODEL
SSM_WIDTH = MIX_WIDTH // 4
SSM_GROUP = 16
SSM_GROUPS = SSM_WIDTH // SSM_GROUP
SSM_STATE = 64
DT_MIN = 0.001
DT_MAX = 0.1
HEAD_DIM = 64
ATT_WIDTH = MIX_WIDTH // 2
N_HEADS = ATT_WIDTH // HEAD_DIM
N_KV = N_HEADS // 4
Q_PER_KV = N_HEADS // N_KV
KV_WIDTH = N_KV * HEAD_DIM
WINDOW = 128
N_PREV_CHUNKS = WINDOW // CHUNK
ATT_SCALE = HEAD_DIM ** -0.5
LRU_WIDTH = MIX_WIDTH // 4
LRU_BLOCKS = 4
LRU_BLOCK = LRU_WIDTH // LRU_BLOCKS
CONV_WIDTH = 4
LRU_C = 8.0
OFF_Q = SSM_WIDTH
OFF_K = OFF_Q + ATT_WIDTH
OFF_V = OFF_K + KV_WIDTH
OFF_LX = OFF_V + KV_WIDTH
OFF_LG = OFF_LX + LRU_WIDTH
IN_WIDTH = OFF_LG + LRU_WIDTH
D_FF = 2816
EPS = 1e-6
NEG_INF = -1e30

kernel_name = 'hymba_s5_swa_rglru_streaming_step'


def _rms(x, g):
    xf = x.astype(jnp.float32)
    y = xf * lax.rsqrt(jnp.mean(xf * xf, axis=-1, keepdims=True) + EPS)
    return (y * g.astype(jnp.float32)).astype(x.dtype)


def _ffn(x, g, w_gate, w_up, w_down):
    h = _rms(x, g)
    return x + 0.5 * ((jax.nn.silu(h @ w_gate) * (h @ w_up)) @ w_down)


def _complex_combine(e1, e2):
    a1r, a1i, b1r, b1i = e1
    a2r, a2i, b2r, b2i = e2
    return (a2r * a1r - a2i * a1i,
            a2r * a1i + a2i * a1r,
            a2r * b1r - a2i * b1i + b2r,
            a2r * b1i + a2i * b1r + b2i)


def _real_combine(e1, e2):
    a1, b1 = e1
    a2, b2 = e2
    return a1 * a2, a2 * b1 + b2


def _s5(u, h0_re, h0_im, a_re, a_im, log_dt, b_re, b_im, c_re, c_im, d, w_glu):
    f32 = jnp.float32
    bn, L, _ = u.shape
    uf = u.astype(f32)
    ug = uf.reshape(bn, L, SSM_GROUPS, SSM_GROUP)
    a_re = a_re.astype(f32)
    a_im = a_im.astype(f32)
    dt = jnp.exp(log_dt.astype(f32))[:, None]
    mag = jnp.exp(a_re * dt)
    lb_re = mag * jnp.cos(a_im * dt)
    lb_im = mag * jnp.sin(a_im * dt)
    den = a_re * a_re + a_im * a_im
    nr = lb_re - 1.0
    k_re = (nr * a_re + lb_im * a_im) / den
    k_im = (lb_im * a_re - nr * a_im) / den
    b_re = b_re.astype(f32)
    b_im = b_im.astype(f32)
    bb_re = k_re[..., None] * b_re - k_im[..., None] * b_im
    bb_im = k_re[..., None] * b_im + k_im[..., None] * b_re
    x_re = jnp.einsum('gph,blgh->blgp', bb_re, ug)
    x_im = jnp.einsum('gph,blgh->blgp', bb_im, ug)
    h0_re = h0_re.astype(f32)
    h0_im = h0_im.astype(f32)
    x_re = x_re.at[:, 0].add(lb_re * h0_re - lb_im * h0_im)
    x_im = x_im.at[:, 0].add(lb_re * h0_im + lb_im * h0_re)
    a_full_re = jnp.broadcast_to(lb_re, x_re.shape)
    a_full_im = jnp.broadcast_to(lb_im, x_re.shape)
    _, _, h_re, h_im = lax.associative_scan(
        _complex_combine, (a_full_re, a_full_im, x_re, x_im), axis=1)
    y = (jnp.einsum('ghp,blgp->blgh', c_re.astype(f32), h_re)
         - jnp.einsum('ghp,blgp->blgh', c_im.astype(f32), h_im))
    y = y.reshape(bn, L, SSM_WIDTH) + d.astype(f32) * uf
    z = jax.nn.gelu(y)
    out = z * jax.nn.sigmoid(z @ w_glu.astype(f32))
    return out, h_re[:, -1], h_im[:, -1]


def _sink_softmax(s, sink):
    sk = jnp.broadcast_to(sink.astype(jnp.float32)[:, :, None, None], s.shape[:-1] + (1,))
    p = jax.nn.softmax(jnp.concatenate([s, sk], axis=-1), axis=-1)
    return p[..., :-1]


def _attn_prompt(q, k, v, sink):
    bn, L, _ = q.shape
    nc = L // CHUNK
    qc = q.reshape(bn, nc, CHUNK, N_KV, Q_PER_KV, HEAD_DIM)
    pad = ((0, 0), (N_PREV_CHUNKS, 0), (0, 0), (0, 0), (0, 0))
    kp = jnp.pad(k.reshape(bn, nc, CHUNK, N_KV, HEAD_DIM), pad)
    vp = jnp.pad(v.reshape(bn, nc, CHUNK, N_KV, HEAD_DIM), pad)
    kb = jnp.concatenate([kp[:, j:j + nc] for j in range(N_PREV_CHUNKS + 1)], axis=2)
    vb = jnp.concatenate([vp[:, j:j + nc] for j in range(N_PREV_CHUNKS + 1)], axis=2)
    chunk_ids = jnp.arange(nc)[:, None] - N_PREV_CHUNKS + jnp.arange(N_PREV_CHUNKS + 1)[None, :]
    valid = jnp.repeat(chunk_ids >= 0, CHUNK, axis=1)
    s = jnp.einsum('bcqhgd,bcshd->bchgqs', qc, kb).astype(jnp.float32) * ATT_SCALE
    s = jnp.where(valid[None, :, None, None, None, :], s, NEG_INF)
    p = _sink_softmax(s, sink)
    o = jnp.einsum('bchgqs,bcshd->bcqhgd', p.astype(vb.dtype), vb)
    return o.reshape(bn, L, ATT_WIDTH)


def _attn_sample(q, k, v, ck, cv, sink):
    bn, S, _ = q.shape
    qh = q.reshape(bn, S, N_KV, Q_PER_KV, HEAD_DIM)
    kk = jnp.concatenate([ck.astype(k.dtype), k.reshape(bn, S, N_KV, HEAD_DIM)], axis=1)
    vv = jnp.concatenate([cv.astype(v.dtype), v.reshape(bn, S, N_KV, HEAD_DIM)], axis=1)
    s = jnp.einsum('bqhgd,bshd->bhgqs', qh, kk).astype(jnp.float32) * ATT_SCALE
    p = _sink_softmax(s, sink)
    o = jnp.einsum('bhgqs,bshd->bqhgd', p.astype(vv.dtype), vv)
    return o.reshape(bn, S, ATT_WIDTH)


def _causal_conv(xb, prev, w, b):
    L = xb.shape[1]
    xp = jnp.concatenate([prev.astype(xb.dtype), xb], axis=1)
    acc = b + w[0] * xp[:, 0:L]
    for t in range(1, CONV_WIDTH):
        acc = acc + w[t] * xp[:, t:t + L]
    return acc, xp[:, -(CONV_WIDTH - 1):]


def _rg_lru(xc, h0, w_a, b_a, w_x, b_x, lam):
    f32 = jnp.float32
    bn, L, _ = xc.shape
    xf = xc.astype(f32)
    xb = xf.reshape(bn, L, LRU_BLOCKS, LRU_BLOCK)
    r = jax.nn.sigmoid(jnp.einsum('blni,nio->blno', xb, w_a.astype(f32)).reshape(bn, L, LRU_WIDTH) + b_a.astype(f32))
    i = jax.nn.sigmoid(jnp.einsum('blni,nio->blno', xb, w_x.astype(f32)).reshape(bn, L, LRU_WIDTH) + b_x.astype(f32))
    log_a = LRU_C * r * jax.nn.log_sigmoid(lam.astype(f32))
    a = jnp.exp(log_a)
    mult = jnp.sqrt(-jnp.expm1(2.0 * log_a))
    bterm = mult * i * xf
    bterm = bterm.at[:, 0].add(a[:, 0] * h0.astype(f32))
    _, h = lax.associative_scan(_real_combine, (a, bterm), axis=1)
    return h, h[:, -1]


def _mixer(h, p, st):
    f32 = jnp.float32
    bn, L, _ = h.shape
    proj = h @ p['w_in']
    u = proj[..., :OFF_Q]
    q = proj[..., OFF_Q:OFF_K]
    k = proj[..., OFF_K:OFF_V]
    v = proj[..., OFF_V:OFF_LX]
    xl = proj[..., OFF_LX:OFF_LG]
    gl = proj[..., OFF_LG:]
    sink = p['attn_sink'].reshape(N_KV, Q_PER_KV)
    if st is None:
        s_re = jnp.zeros((bn, SSM_GROUPS, SSM_STATE), f32)
        s_im = jnp.zeros((bn, SSM_GROUPS, SSM_STATE), f32)
        conv_prev = jnp.zeros((bn, CONV_WIDTH - 1, LRU_WIDTH), h.dtype)
        lru_h = jnp.zeros((bn, LRU_WIDTH), f32)
        y_att = _attn_prompt(q, k, v, sink)
        rows = min(WINDOW, L)
        new_k = k[:, L - rows:].reshape(bn, rows, N_KV, HEAD_DIM)
        new_v = v[:, L - rows:].reshape(bn, rows, N_KV, HEAD_DIM)
    else:
        ck, cv, s_re, s_im, conv_prev, lru_h = st
        y_att = _attn_sample(q, k, v, ck, cv, sink)
        new_k = k.reshape(bn, L, N_KV, HEAD_DIM)
        new_v = v.reshape(bn, L, N_KV, HEAD_DIM)
    y_ssm, n_re, n_im = _s5(u, s_re, s_im, p['ssm_a_re'], p['ssm_a_im'], p['ssm_log_dt'],
                            p['ssm_b_re'], p['ssm_b_im'], p['ssm_c_re'], p['ssm_c_im'],
                            p['ssm_d'], p['ssm_w_glu'])
    xc, new_conv = _causal_conv(xl, conv_prev, p['conv_w'], p['conv_b'])
    hl, new_lru = _rg_lru(xc, lru_h, p['lru_w_a'], p['lru_b_a'], p['lru_w_x'], p['lru_b_x'], p['lru_lambda'])
    y_lru = jax.nn.gelu(gl.astype(f32)) * hl
    g = p['out_norm']
    y = jnp.concatenate([
        _rms(y_ssm.astype(f32), g[:SSM_WIDTH]),
        _rms(y_att.astype(f32), g[SSM_WIDTH:SSM_WIDTH + ATT_WIDTH]),
        _rms(y_lru, g[SSM_WIDTH + ATT_WIDTH:]),
    ], axis=-1).astype(h.dtype)
    return y @ p['w_out'], (new_k, new_v, n_re, n_im, new_conv, new_lru)


def setup_inputs(seed: int = 0) -> dict:
    key = jax.random.key(seed)
    keys = iter(jax.random.split(key, 64))

    def nrm(shape, scale):
        return scale * jax.random.normal(next(keys), shape, jnp.float32)

    def unif(shape, lo, hi):
        return jax.random.uniform(next(keys), shape, jnp.float32, lo, hi)

    cache_len = min(WINDOW, PAST_LEN)
    n = jnp.arange(SSM_STATE, dtype=jnp.float32)
    lru_target = unif((DEPTH, LRU_WIDTH), 0.9, 0.999)
    lru_base = lru_target ** (1.0 / LRU_C)
    return {
        'x_prompt': nrm((BATCH, SEQ, D_MODEL), 1.0),
        'x_sample': nrm((DEC_BATCH, DEC_SEQ, D_MODEL), 1.0),
        'cache_k': nrm((DEPTH, DEC_BATCH, cache_len, N_KV, HEAD_DIM), 1.0),
        'cache_v': nrm((DEPTH, DEC_BATCH, cache_len, N_KV, HEAD_DIM), 1.0),
        'state_ssm_re': nrm((DEPTH, DEC_BATCH, SSM_GROUPS, SSM_STATE), 0.1),
        'state_ssm_im': nrm((DEPTH, DEC_BATCH, SSM_GROUPS, SSM_STATE), 0.1),
        'state_conv': nrm((DEPTH, DEC_BATCH, CONV_WIDTH - 1, LRU_WIDTH), 1.0),
        'state_lru': nrm((DEPTH, DEC_BATCH, LRU_WIDTH), 0.5),
        'ffn1_norm': 1.0 + nrm((DEPTH, D_MODEL), 0.01),
        'ffn1_w_gate': nrm((DEPTH, D_MODEL, D_FF), D_MODEL ** -0.5),
        'ffn1_w_up': nrm((DEPTH, D_MODEL, D_FF), D_MODEL ** -0.5),
        'ffn1_w_down': nrm((DEPTH, D_FF, D_MODEL), D_FF ** -0.5),
        'mix_norm': 1.0 + nrm((DEPTH, D_MODEL), 0.01),
        'w_in': nrm((DEPTH, D_MODEL, IN_WIDTH), D_MODEL ** -0.5),
        'ssm_a_re': -0.5 + nrm((DEPTH, SSM_GROUPS, SSM_STATE), 0.01),
        'ssm_a_im': math.pi * n + nrm((DEPTH, SSM_GROUPS, SSM_STATE), 0.01),
        'ssm_log_dt': unif((DEPTH, SSM_GROUPS), math.log(DT_MIN), math.log(DT_MAX)),
        'ssm_b_re': nrm((DEPTH, SSM_GROUPS, SSM_STATE, SSM_GROUP), (2 * SSM_GROUP) ** -0.5),
        'ssm_b_im': nrm((DEPTH, SSM_GROUPS, SSM_STATE, SSM_GROUP), (2 * SSM_GROUP) ** -0.5),
        'ssm_c_re': nrm((DEPTH, SSM_GROUPS, SSM_GROUP, SSM_STATE), SSM_STATE ** -0.5),
        'ssm_c_im': nrm((DEPTH, SSM_GROUPS, SSM_GROUP, SSM_STATE), SSM_STATE ** -0.5),
        'ssm_d': nrm((DEPTH, SSM_WIDTH), 0.5),
        'ssm_w_glu': nrm((DEPTH, SSM_WIDTH, SSM_WIDTH), SSM_WIDTH ** -0.5),
        'attn_sink': nrm((DEPTH, N_HEADS), 0.5),
        'conv_w': nrm((DEPTH, CONV_WIDTH, LRU_WIDTH), CONV_WIDTH ** -0.5),
        'conv_b': nrm((DEPTH, LRU_WIDTH), 0.01),
        'lru_w_a': nrm((DEPTH, LRU_BLOCKS, LRU_BLOCK, LRU_BLOCK), LRU_BLOCK ** -0.5),
        'lru_b_a': nrm((DEPTH, LRU_WIDTH), 0.01),
        'lru_w_x': nrm((DEPTH, LRU_BLOCKS, LRU_BLOCK, LRU_BLOCK), LRU_BLOCK ** -0.5),
        'lru_b_x': nrm((DEPTH, LRU_WIDTH), 0.01),
        'lru_lambda': jnp.log(lru_base) - jnp.log1p(-lru_base),
        'out_norm': 1.0 + nrm((DEPTH, MIX_WIDTH), 0.01),
        'w_out': nrm((DEPTH, MIX_WIDTH, D_MODEL), MIX_WIDTH ** -0.5),
        'ffn2_norm': 1.0 + nrm((DEPTH, D_MODEL), 0.01),
        'ffn2_w_gate': nrm((DEPTH, D_MODEL, D_FF), D_MODEL ** -0.5),
        'ffn2_w_up': nrm((DEPTH, D_MODEL, D_FF), D_MODEL ** -0.5),
        'ffn2_w_down': nrm((DEPTH, D_FF, D_MODEL), D_FF ** -0.5),
        'final_norm': 1.0 + nrm((D_MODEL,), 0.01),
    }


def reference(x_prompt, x_sample, cache_k, cache_v, state_ssm_re, state_ssm_im, state_conv, state_lru,
              ffn1_norm, ffn1_w_gate, ffn1_w_up, ffn1_w_down, mix_norm, w_in,
              ssm_a_re, ssm_a_im, ssm_log_dt, ssm_b_re, ssm_b_im, ssm_c_re, ssm_c_im, ssm_d, ssm_w_glu,
              attn_sink, conv_w, conv_b, lru_w_a, lru_b_a, lru_w_x, lru_b_x, lru_lambda,
              out_norm, w_out, ffn2_norm, ffn2_w_gate, ffn2_w_up, ffn2_w_down, final_norm):

    def run(x, states_in):
        collected = [[], [], [], [], [], []]
        for l in range(DEPTH):
            p = {
                'w_in': w_in[l], 'ssm_a_re': ssm_a_re[l], 'ssm_a_im': ssm_a_im[l],
                'ssm_log_dt': ssm_log_dt[l], 'ssm_b_re': ssm_b_re[l], 'ssm_b_im': ssm_b_im[l],
                'ssm_c_re': ssm_c_re[l], 'ssm_c_im': ssm_c_im[l], 'ssm_d': ssm_d[l],
                'ssm_w_glu': ssm_w_glu[l], 'attn_sink': attn_sink[l], 'conv_w': conv_w[l],
                'conv_b': conv_b[l], 'lru_w_a': lru_w_a[l], 'lru_b_a': lru_b_a[l],
                'lru_w_x': lru_w_x[l], 'lru_b_x': lru_b_x[l], 'lru_lambda': lru_lambda[l],
                'out_norm': out_norm[l], 'w_out': w_out[l],
            }
            x = _ffn(x, ffn1_norm[l], ffn1_w_gate[l], ffn1_w_up[l], ffn1_w_down[l])
            st = None if states_in is None else tuple(s[l] for s in states_in)
            m, new = _mixer(_rms(x, mix_norm[l]), p, st)
            x = x + m
            x = _ffn(x, ffn2_norm[l], ffn2_w_gate[l], ffn2_w_up[l], ffn2_w_down[l])
            for lst, s in zip(collected, new):
                lst.append(s)
        return _rms(x, final_norm), [jnp.stack(c, axis=0) for c in collected]

    y_prompt, (k_p, v_p, sre_p, sim_p, conv_p, lru_p) = run(x_prompt, None)
    y_sample, (k_s, v_s, sre_s, sim_s, conv_s, lru_s) = run(
        x_sample, (cache_k, cache_v, state_ssm_re, state_ssm_im, state_conv, state_lru))
    return (y_prompt, y_sample, k_p, v_p, sre_p, sim_p, conv_p, lru_p,
            k_s, v_s, sre_s, sim_s, conv_s, lru_s)
```

```python
import math
import numpy as np
import concourse.bass as bass
import concourse.mybir as mybir
from concourse.bass_utils import run_bass_kernel_spmd

F32 = mybir.dt.float32
BF16 = mybir.dt.bfloat16
I32 = mybir.dt.int32
AF = mybir.ActivationFunctionType
ALU = mybir.AluOpType

D = 1024
DFF = 2816
NFF = 22
SEQ = 4096
TT = 512
NPT = SEQ // TT
NS = 4
LS = 16
TS = 128
EPS = 1e-6
OFF_Q, OFF_K, OFF_V, OFF_LX, OFF_LG = 256, 768, 896, 1024, 1280
TWO_PI = 2.0 * math.pi


class _Stop(Exception):
    pass


class Trk:
    def __init__(self, nc):
        self.nc = nc
        self.eng = {"pe": nc.tensor, "dve": nc.vector, "act": nc.scalar, "pool": nc.gpsimd, "sp": nc.sync}
        self.sem, self.cnt, self.waited, self.lastw, self.readers = {}, {}, {}, {}, {}
        self._stack = []
        for e in self.eng:
            self.newsem(e)

    def newsem(self, name):
        cm = self.nc.semaphore(name)
        self._stack.append(cm)
        self.sem[name] = cm.__enter__()
        self.cnt[name] = 0
        return name

    def _wait(self, e, toks):
        need = {}
        for (s, c) in toks:
            if c > need.get(s, 0):
                need[s] = c
        for s, c in need.items():
            if self.waited.get((e, s), 0) >= c:
                continue
            if s == e and e == "pe":
                continue
            self.eng[e].wait_ge(self.sem[s], c)
            self.waited[(e, s)] = c

    def deps(self, reads, writes):
        toks = []
        for r in reads:
            if r in self.lastw:
                toks.append(self.lastw[r])
        for w in writes:
            if w in self.lastw:
                toks.append(self.lastw[w])
            toks.extend(self.readers.get(w, []))
        return toks

    def commit(self, tok, reads, writes):
        for r in reads:
            self.readers.setdefault(r, []).append(tok)
        for w in writes:
            self.lastw[w] = tok
            self.readers[w] = []

    def op(self, e, fn, reads=(), writes=(), signal=True):
        self._wait(e, self.deps(reads, writes))
        inst = fn()
        if signal:
            self.cnt[e] += 1
            inst.then_inc(self.sem[e], 1)
            tok = (e, self.cnt[e])
        else:
            tok = (e, self.cnt[e] + 1)
        self.commit(tok, reads, writes)
        return tok

    def dma(self, q, out, in_, reads=(), writes=(), sem=None, **kw):
        self._wait(q, self.deps(reads, writes))
        if sem is None or sem in ("ld", "st", "ld2"):
            if not hasattr(self, "pool"):
                self.pool = [self.newsem(f"dp{i}") for i in range(32)]
                self.pool_i = 0
            sem = self.pool[self.pool_i % len(self.pool)]
            self.pool_i += 1
            if self.cnt[sem] > 0:
                self._wait(q, [(sem, self.cnt[sem])])
        inst = self.eng[q].dma_start(out=out, in_=in_, **kw)
        self.cnt[sem] += 16
        inst.then_inc(self.sem[sem], 16)
        tok = (sem, self.cnt[sem])
        self.commit(tok, reads, writes)
        return tok


WNAMES = ["ffn1_norm", "ffn1_w_gate", "ffn1_w_up", "ffn1_w_down", "mix_norm", "w_in",
          "ssm_a_re", "ssm_a_im", "ssm_log_dt", "ssm_b_re", "ssm_b_im", "ssm_c_re", "ssm_c_im", "ssm_d",
          "ssm_w_glu", "attn_sink", "conv_w", "conv_b", "lru_w_a", "lru_b_a", "lru_w_x", "lru_b_x",
          "lru_lambda", "out_norm", "w_out", "ffn2_norm", "ffn2_w_gate", "ffn2_w_up", "ffn2_w_down",
          "final_norm"]
WSHAPES = {
    "ffn1_norm": (2, D), "ffn1_w_gate": (2, D, DFF), "ffn1_w_up": (2, D, DFF), "ffn1_w_down": (2, DFF, D),
    "mix_norm": (2, D), "w_in": (2, D, 1536), "ssm_a_re": (2, 16, 64), "ssm_a_im": (2, 16, 64),
    "ssm_log_dt": (2, 16), "ssm_b_re": (2, 16, 64, 16), "ssm_b_im": (2, 16, 64, 16),
    "ssm_c_re": (2, 16, 16, 64), "ssm_c_im": (2, 16, 16, 64), "ssm_d": (2, 256), "ssm_w_glu": (2, 256, 256),
    "attn_sink": (2, 8), "conv_w": (2, 4, 256), "conv_b": (2, 256), "lru_w_a": (2, 4, 64, 64),
    "lru_b_a": (2, 256), "lru_w_x": (2, 4, 64, 64), "lru_b_x": (2, 256), "lru_lambda": (2, 256),
    "out_norm": (2, D), "w_out": (2, D, D), "ffn2_norm": (2, D), "ffn2_w_gate": (2, D, DFF),
    "ffn2_w_up": (2, D, DFF), "ffn2_w_down": (2, DFF, D), "final_norm": (D,),
}
OUT_SHAPES = {
    "yp": (SEQ, D), "ys": (NS * LS, D), "kp": (2, 128, 128), "vp": (2, 128, 128),
    "srep": (2, 16, 64), "simp": (2, 16, 64), "convp": (2, 3, 256), "lrup": (2, 256),
    "ks": (2, NS, LS, 128), "vs": (2, NS, LS, 128), "sres": (2, NS, 16, 64), "sims": (2, NS, 16, 64),
    "convs": (2, NS, 3, 256), "lrus": (2, NS, 256),
}


def build_program(n_ptiles=NPT, do_sample=True, stop=None):
    nc = bass.Bass("TRN2", target_bir_lowering=False)
    t = Trk(nc)

    def din(name, shape):
        return nc.dram_tensor(name, list(shape), F32, kind="ExternalInput").ap()

    def dout(name, shape):
        return nc.dram_tensor(name, list(shape), F32, kind="ExternalOutput").ap()

    I = {}
    I["xp"] = din("xp", (SEQ, D))
    I["xs"] = din("xs", (NS * LS, D))
    I["ck"] = din("ck", (2, NS, 128, 128))
    I["cv"] = din("cv", (2, NS, 128, 128))
    I["sre"] = din("sre", (2, NS, 16, 64))
    I["sim"] = din("sim", (2, NS, 16, 64))
    I["sconv"] = din("sconv", (2, NS, 3, 256))
    I["slru"] = din("slru", (2, NS, 256))
    for n in WNAMES:
        I[n] = din(n, WSHAPES[n])
    O = {n: dout(n, s) for n, s in OUT_SHAPES.items()}
    if stop is not None:
        O["dbg"] = dout("dbg", (128, 8192))
    dbg_off = [0]

    def dump(ap2d, key, width):
        dst = O["dbg"][:, dbg_off[0]:dbg_off[0] + width]
        if len(ap2d.shape) == 3:
            dst = dst.rearrange("p (c t) -> p c t", c=ap2d.shape[1])
        t.dma("sp", dst, ap2d, reads=key, sem=st)
        dbg_off[0] += width

    def finish():
        for sname in ring_sem + getattr(t, "pool", []):
            if t.cnt[sname] > 0:
                nc.sync.wait_ge(t.sem[sname], t.cnt[sname])
        return nc
    s5scr_t = nc.dram_tensor("s5scr_t", [2, 128, 4096], F32, kind="Internal").ap()
    s5scr_w = nc.dram_tensor("s5scr_w", [2, 128, 8192], BF16, kind="Internal").ap()

    def sb(name, shape, dt=F32):
        return nc.sbuf_tensor(name, list(shape), dt).__enter__()

    def dsem(name):
        return t.newsem(name)

    _once = {}

    def sb_once(name, shape, dt=F32):
        if name not in _once:
            _once[name] = sb(name, shape, dt)
        return _once[name]

    xT = sb("xT", [128, 8, TT])
    hT = sb("hT", [128, 8, TT], BF16)
    act = sb("act", [128, NFF, TT], BF16)
    ycat = sb("ycat", [128, 8, TT])
    R = 5
    RSZ = 2816
    ring = [sb(f"ring{i}", [128, RSZ], BF16) for i in range(R)]
    ring_sem = [dsem(f"rs{i}") for i in range(R)]
    s5t = sb("s5t", [128, 2, 16, TS])
    s5w = sb("s5w", [128, 4, 16, 128], BF16)
    xtm2 = sb("xtm2", [128, 2, D])
    xtm = [xtm2[:, 0, :], xtm2[:, 1, :]]
    sqb = [sb(f"sqb{i}", [128, TT], BF16) for i in range(2)]
    rstd = sb("rstd", [128, TT])
    rs1 = rstd
    sg = [sb(f"sg{i}", [128, TT]) for i in range(2)]
    uT = sb("uT", [128, 2, TT])
    ubf = sb("ubf", [128, 2, TT], BF16)
    qT = sb("qT", [128, 4, TT], BF16)
    kkT = sb("kkT", [128, 2, 128 + TT], BF16)
    vtm = sb("vtm", [128, 1 + TT // 128, 256], BF16)
    vnew = sb("vnew", [16, NS, 256], BF16)
    xlT = sb("xlT", [128, 2, 3 + TT])
    gg = sb("gg", [128, 2, TT])
    pT = [sb(f"pT{i}", [128, 2, 256], BF16) for i in range(2)]
    dn_sb = [sb(f"dnsb{i}", [128, 256]) for i in range(2)]
    a32 = sb("a32", [128, 16])
    b32 = sb("b32", [128, 16])
    zb = sb("zb", [128, 2, TT], BF16)
    sig = sb("sig", [128, TT])
    xc = sb("xc", [128, TT])
    xcb = sb("xcb", [128, TT], BF16)
    lrut = sb("lrut", [128, 4, TT])
    rr, ii, aa, mm = lrut[:, 0, :], lrut[:, 1, :], lrut[:, 2, :], lrut[:, 3, :]
    decb = xtm2[:, :, :].rearrange("p a t -> p (a t)")
    kvx = sb("kvx", [128, 512])
    ckf = sb("ckf", [128, 128])
    ckd = sb("ckd", [128, 2, 128])
    cvf = sb("cvf", [128, 128])
    ckT = sb("ckT", [128, 2, 128], BF16)
    hs_p = sb("hs_p", [128, 2, 16])
    hs_s = sb("hs_s", [128, 2, NS, 16])
    hcur = sb("hcur", [128, 16])
    lru_p = sb("lru_p", [128, 2, 2])
    lru_s = sb("lru_s", [128, 2, NS, 2])
    convh_p = sb("convh_p", [128, 2, 2, 3])
    kk_halo = sb("kk_halo", [128, 2, 2, 128], BF16)
    v_halo = sb("v_halo", [128, 2, 256], BF16)
    idn = sb("idn", [128, 128])
    idn2 = sb("idn2", [128, 128])
    iof = sb("iof", [128, 128])
    ones_m = sb("ones_m", [128, 3, 128], BF16)
    ones_b = sb("ones_b", [128, 128], BF16)
    Pm = sb("Pm", [128, 128])
    iot = sb("iot", [128, TS])
    gN = sb("gN", [128, 2, 4, 8])
    gF = sb("gF", [128, 8])
    dD = sb("dD", [128, 2, 2])
    cw = sb("cw", [128, 2, 4, 2])
    cb = sb("cb", [128, 2, 2])
    ba = sb("ba", [128, 2, 2])
    bx = sb("bx", [128, 2, 2])
    lam = sb("lam", [128, 2, 2])
    c1 = sb("c1", [128, 2, 2])
    c2 = sb("c2", [128, 2, 2])
    esk = sb("esk", [128, 2, 8])
    esink = sb("esink", [128, 2, 2, 256])
    wglu = sb("wglu", [128, 2, 2, 256], BF16)
    wlru = sb("wlru", [128, 2, 4, 128], BF16)
    rdec = sb("rdec", [128, 2, 16])
    pa_re = sb("pa_re", [128, 16])
    pa_im = sb("pa_im", [128, 16])
    pdt = sb("pdt", [128, 16])
    tmpA = sb("tmpA", [128, 16])
    tmpB = sb("tmpB", [128, 16])
    tmpC = sb("tmpC", [128, 16])
    tmpD = sb("tmpD", [128, 16])
    tmpi = sb("tmpi", [128, 16], I32)
    lbre = sb("lbre", [128, 16])
    lbim = sb("lbim", [128, 16])
    kre = sb("kre", [128, 16])
    kim = sb("kim", [128, 16])
    K1 = sb("K1", [128, 16])
    K2 = sb("K2", [128, 16])
    K3 = sb("K3", [128, 16])
    K4 = sb("K4", [128, 16])
    actf = act[:, :, :].rearrange("p j t -> p (j t)").bitcast(F32)
    zbuf = actf[:, 0:2048]
    wbuf = actf[:, 2048:4096]
    tbuf = actf[:, 4096:5120]
    hTf = hT[:, :, :].rearrange("p c t -> p (c t)")
    Abuf = hTf[:, 0:2048]
    Bbuf = hTf[:, 2048:4096]
    bre2 = actf[:, 0:256].rearrange("p (g h) -> p g h", g=16)
    bim2 = actf[:, 256:512].rearrange("p (g h) -> p g h", g=16)
    BB = actf[:, 512:1024].rearrange("p (a f) -> p a f", a=2)
    BBt = actf[:, 1024:1280]
    cc_in = actf[:, 1280:1536].rearrange("p (c r q) -> p c r q", c=2, r=2)
    CC = actf[:, 1536:1792].rearrange("p (a f) -> p a f", a=2)

    pX = nc.psum_tensor("pX", [128, 2048], F32).__enter__()
    PS = {n: nc.psum_tensor(n, [128, 512], F32).__enter__() for n in ["pO0", "pO1", "pM0", "pM1"]}
    for i_, n in enumerate(["pA0", "pA1", "pB0", "pB1"]):
        PS[n] = pX[:, i_ * 512:(i_ + 1) * 512]
    rot = {"pA": 0, "pB": 0, "pO": 0, "pM": 0}

    def bank(kind):
        i = rot[kind]
        rot[kind] ^= 1
        n = f"{kind}{i}"
        return PS[n], n

    V, A_, P_, PE_ = nc.vector, nc.scalar, nc.gpsimd, nc.tensor
    ld = dsem("ld")
    st = dsem("st")

    def mm_(out, lhsT, rhs, start, stop, reads, writes, signal):
        t.op("pe", lambda: PE_.matmul(out, lhsT=lhsT, rhs=rhs, start=start, stop=stop), reads=reads, writes=writes,
             signal=signal)

    t.op("pool", lambda: P_.iota(iof[:], pattern=[[1, 128]], base=0, channel_multiplier=-1,
                                 allow_small_or_imprecise_dtypes=True), writes=["iof"])
    t.op("dve", lambda: V.tensor_single_scalar(idn[:], iof[:], 0.0, ALU.is_equal), reads=["iof"], writes=["idn"])
    t.op("dve", lambda: V.tensor_single_scalar(Pm[:], iof[:], 64.0, ALU.is_equal), reads=["iof"], writes=["Pm"])
    t.op("dve", lambda: V.tensor_single_scalar(idn2[:], iof[:], -64.0, ALU.is_equal), reads=["iof"], writes=["idn2"])
    t.op("dve", lambda: V.tensor_tensor(Pm[:], Pm[:], idn2[:], ALU.subtract), reads=["Pm", "idn2"], writes=["Pm"])
    t.op("pool", lambda: P_.iota(iot[:], pattern=[[1, TS]], base=1, channel_multiplier=0,
                                 allow_small_or_imprecise_dtypes=True), writes=["iot"])
    for i, v in enumerate([1.0 / 1024, 1.0 / 512, 1.0 / 256]):
        t.op("dve", lambda i=i, v=v: V.memset(ones_m[:, i, :], v), writes=[("ones_m", i)])
    t.op("dve", lambda: V.memset(ones_b[:], 1.0), writes=["ones_b"])
    for nm in ["hs_p", "lru_p", "convh_p"]:
        pass
    t.op("dve", lambda: V.memset(hs_p[:], 0.0), writes=[("hs_p", 0), ("hs_p", 1)])
    t.op("dve", lambda: V.memset(lru_p[:], 0.0), writes=[("lru_p", 0), ("lru_p", 1)])
    t.op("dve", lambda: V.memset(convh_p[:], 0.0), writes=[("convh_p", 0), ("convh_p", 1)])
    t.op("dve", lambda: V.memset(wlru[:], 0.0), writes=["wlru"])
    ld2 = dsem("ld2")

    def ldc(dst, src, wkey, **kw):
        t.dma("sp", dst, src, writes=[wkey], sem=ld, **kw)

    NC_ = dict(allow_slow_non_contiguous=True)
    for l in range(2):
        for k, nm in enumerate(["ffn1_norm", "mix_norm", "out_norm", "ffn2_norm"]):
            ldc(gN[:, l, k, :], I[nm][l].rearrange("(c p) -> p c", p=128), "gN", **NC_)
        ldc(dD[:, l, :], I["ssm_d"][l].rearrange("(c p) -> p c", p=128), "dD", **NC_)
        for tt_ in range(4):
            ldc(cw[:, l, tt_, :], I["conv_w"][l, tt_].rearrange("(c p) -> p c", p=128), "cw", **NC_)
        ldc(cb[:, l, :], I["conv_b"][l].rearrange("(c p) -> p c", p=128), "cb", **NC_)
        ldc(ba[:, l, :], I["lru_b_a"][l].rearrange("(c p) -> p c", p=128), "ba", **NC_)
        ldc(bx[:, l, :], I["lru_b_x"][l].rearrange("(c p) -> p c", p=128), "bx", **NC_)
        ldc(lam[:, l, :], I["lru_lambda"][l].rearrange("(c p) -> p c", p=128), "lam", **NC_)
        ldc(esk[:, l, :], I["attn_sink"][l:l + 1, :].to_broadcast([128, 8]), "esk", **NC_)
        t.dma("pool", wglu[:, l, :, :], I["ssm_w_glu"][l].rearrange("(c p) f -> p c f", p=128), writes=["wglu"], sem=ld2)
        for wi_, nm in enumerate(["lru_w_a", "lru_w_x"]):
            for n_ in range(4):
                c_, hlf = n_ // 2, n_ % 2
                t.dma("pool", wlru[hlf * 64:(hlf + 1) * 64, l, wi_ * 2 + c_, hlf * 64:(hlf + 1) * 64], I[nm][l, n_], writes=["wlru"], sem=ld2)
    ldc(gF[:], I["final_norm"].rearrange("(c p) -> p c", p=128), "gF", **NC_)
    t.op("act", lambda: A_.activation(c1[:], lam[:], AF.Exp, scale=-1.0), reads=["lam"], writes=["c1"])
    t.op("act", lambda: A_.activation(c1[:], c1[:], AF.Ln, bias=1.0, scale=1.0), reads=["c1"], writes=["c1"])
    t.op("dve", lambda: V.tensor_scalar(c2[:], c1[:], -16.0, None, ALU.mult), reads=["c1"], writes=["c2"])
    t.op("dve", lambda: V.tensor_scalar(c1[:], c1[:], -8.0, None, ALU.mult), reads=["c1", "c2"], writes=["c1"])
    t.op("act", lambda: A_.activation(esk[:], esk[:], AF.Exp), reads=["esk"], writes=["esk"])
    for l in range(2):
        for h in range(2):
            for gp in range(2):
                for i in range(2):
                    g = 2 * i + gp
                    t.op("dve", lambda l=l, h=h, gp=gp, i=i, g=g: V.tensor_copy(
                        esink[:, l, h, gp * 128 + i * 64: gp * 128 + i * 64 + 64],
                        esk[:, l, 4 * h + g: 4 * h + g + 1].to_broadcast([128, 64])), reads=["esk"], writes=["esink"])

    esbP = sb("esbP", [1, 2, 2, 256], BF16)
    esbS = sb("esbS", [1, 2, 2, 64], BF16)
    t.op("dve", lambda: V.tensor_copy(esbP[:], esink[0:1, :, :, :]), reads=["esink"], writes=["esb"])
    for gp in range(2):
        for i in range(2):
            t.op("dve", lambda gp=gp, i=i: V.tensor_copy(esbS[:, :, :, gp * 32 + i * 16: gp * 32 + i * 16 + 16],
                                                         esink[0:1, :, :, gp * 128 + i * 64: gp * 128 + i * 64 + 16]), reads=["esink"], writes=["esb"])

    def build_s5(l):
        for hf in range(2):
            ps_ = slice(hf * 64, hf * 64 + 64)
            ldc(pa_re[ps_, :], I["ssm_a_re"][l].rearrange("g p -> p g"), "pa_re", **NC_)
            ldc(pa_im[ps_, :], I["ssm_a_im"][l].rearrange("g p -> p g"), "pa_im", **NC_)
            ldc(bre2[ps_, :, :], I["ssm_b_re"][l].rearrange("g p h -> p g h"), "bre2", **NC_)
            ldc(bim2[ps_, :, :], I["ssm_b_im"][l].rearrange("g p h -> p g h"), "bim2", **NC_)
        ldc(pdt[:], I["ssm_log_dt"][l:l + 1, :].to_broadcast([128, 16]), "pdt", **NC_)
        for ch in range(2):
            ldc(cc_in[:, ch, 0, :], I["ssm_c_re"][l, 8 * ch:8 * ch + 8].rearrange("g h p -> (g h) p"), "cc_in")
            ldc(cc_in[:, ch, 1, :], I["ssm_c_im"][l, 8 * ch:8 * ch + 8].rearrange("g h p -> (g h) p"), "cc_in")
        dv = lambda fn, r, w: t.op("dve", fn, reads=r, writes=w)
        ac = lambda fn, r, w: t.op("act", fn, reads=r, writes=w)
        ac(lambda: A_.activation(pdt[:], pdt[:], AF.Exp), ["pdt"], ["pdt"])
        dv(lambda: V.tensor_tensor(tmpA[:], pa_re[:], pdt[:], ALU.mult), ["pa_re", "pdt"], ["tmpA"])
        ac(lambda: A_.activation(rdec[:, l, :], tmpA[:], AF.Exp), ["tmpA"], [("rdec", l)])
        dv(lambda: V.tensor_tensor(tmpB[:], pa_im[:], pdt[:], ALU.mult), ["pa_im", "pdt"], ["tmpB"])
        dv(lambda: V.tensor_scalar(tmpB[:], tmpB[:], 1.0 / TWO_PI, None, ALU.mult), ["tmpB"], ["tmpB"])

        def sincos(dst, dkey, shift):
            dv(lambda: V.tensor_scalar(tmpC[:], tmpB[:], shift, None, ALU.add), ["tmpB"], ["tmpC"])
            dv(lambda: V.tensor_copy(tmpi[:], tmpC[:]), ["tmpC"], ["tmpi"])
            dv(lambda: V.tensor_copy(tmpD[:], tmpi[:]), ["tmpi"], ["tmpD"])
            dv(lambda: V.tensor_tensor(tmpC[:], tmpC[:], tmpD[:], ALU.subtract), ["tmpC", "tmpD"], ["tmpC"])
            ac(lambda: A_.activation(dst, tmpC[:], AF.Sin, scale=TWO_PI), ["tmpC"], [dkey])

        sincos(lbim[:], "lbim_k", 0.0)
        sincos(lbre[:], "lbre_k", 0.25)
        dv(lambda: V.tensor_tensor(lbre[:], lbre[:], rdec[:, l, :], ALU.mult), ["lbre_k", ("rdec", l)], ["lbre_k"])
        dv(lambda: V.tensor_tensor(lbim[:], lbim[:], rdec[:, l, :], ALU.mult), ["lbim_k", ("rdec", l)], ["lbim_k"])
        dv(lambda: V.tensor_tensor(tmpA[:], pa_re[:], pa_re[:], ALU.mult), ["pa_re"], ["tmpA"])
        dv(lambda: V.tensor_tensor(tmpC[:], pa_im[:], pa_im[:], ALU.mult), ["pa_im"], ["tmpC"])
        dv(lambda: V.tensor_tensor(tmpA[:], tmpA[:], tmpC[:], ALU.add), ["tmpA", "tmpC"], ["tmpA"])
        dv(lambda: V.reciprocal(tmpA[:], tmpA[:]), ["tmpA"], ["tmpA"])
        dv(lambda: V.tensor_scalar(tmpC[:], lbre[:], -1.0, None, ALU.add), ["lbre_k"], ["tmpC"])
        dv(lambda: V.tensor_tensor(kre[:], tmpC[:], pa_re[:], ALU.mult), ["tmpC", "pa_re"], ["kre"])
        dv(lambda: V.tensor_tensor(tmpD[:], lbim[:], pa_im[:], ALU.mult), ["lbim_k", "pa_im"], ["tmpD"])
        dv(lambda: V.tensor_tensor(kre[:], kre[:], tmpD[:], ALU.add), ["kre", "tmpD"], ["kre"])
        dv(lambda: V.tensor_tensor(kre[:], kre[:], tmpA[:], ALU.mult), ["kre", "tmpA"], ["kre"])
        dv(lambda: V.tensor_tensor(kim[:], lbim[:], pa_re[:], ALU.mult), ["lbim_k", "pa_re"], ["kim"])
        dv(lambda: V.tensor_tensor(tmpD[:], tmpC[:], pa_im[:], ALU.mult), ["tmpC", "pa_im"], ["tmpD"])
        dv(lambda: V.tensor_tensor(kim[:], kim[:], tmpD[:], ALU.subtract), ["kim", "tmpD"], ["kim"])
        dv(lambda: V.tensor_tensor(kim[:], kim[:], tmpA[:], ALU.mult), ["kim", "tmpA"], ["kim"])
        lo, hi = slice(0, 64), slice(64, 128)
        dv(lambda: V.tensor_copy(K1[lo, :], kre[lo, :]), ["kre"], ["K1"])
        dv(lambda: V.tensor_copy(K1[hi, :], kim[hi, :]), ["kim"], ["K1"])
        dv(lambda: V.tensor_scalar(K2[lo, :], kim[lo, :], -1.0, None, ALU.mult), ["kim"], ["K2"])
        dv(lambda: V.tensor_copy(K2[hi, :], kre[hi, :]), ["kre"], ["K2"])
        dv(lambda: V.tensor_copy(K3[lo, :], kim[lo, :]), ["kim"], ["K3"])
        dv(lambda: V.tensor_scalar(K3[hi, :], kre[hi, :], -1.0, None, ALU.mult), ["kre"], ["K3"])
        dv(lambda: V.tensor_copy(K4[lo, :], kre[lo, :]), ["kre"], ["K4"])
        dv(lambda: V.tensor_copy(K4[hi, :], kim[hi, :]), ["kim"], ["K4"])
        bcast = lambda k: k[:, :].unsqueeze(2).to_broadcast([128, 16, 16])
        for which, (Ka, Kb) in enumerate([(K1, K2), (K3, K4)]):
            BBv = BB[:, which, :].rearrange("p (g h) -> p g h", g=16)
            BBtv = BBt[:, :].rearrange("p (g h) -> p g h", g=16)
            dv(lambda BBv=BBv, Ka=Ka: V.tensor_tensor(BBv, bre2[:], bcast(Ka), ALU.mult), ["bre2", "K1", "K2", "K3", "K4"], [("BB", which)])
            dv(lambda BBtv=BBtv, Kb=Kb: V.tensor_tensor(BBtv, bim2[:], bcast(Kb), ALU.mult), ["bim2", "K1", "K2", "K3", "K4"], ["BBt"])
            dv(lambda which=which: V.tensor_tensor(BB[:, which, :], BB[:, which, :], BBt[:], ALU.add), [("BB", which), "BBt"], [("BB", which)])
        dv(lambda: V.memset(s5w[:], 0.0), [], ["s5w"] + [("s5w", a_, g_) for a_ in range(4) for g_ in range(16)])
        for which in range(2):
            for ch in range(2):
                pt_, pk = bank("pM")
                t.op("pe", lambda pt_=pt_, which=which, ch=ch: PE_.transpose(pt_[:, 0:128], BB[:, which, ch * 128:(ch + 1) * 128], idn[:]),
                     reads=[("BB", which), "idn"], writes=[pk])
                for g8 in range(8):
                    g = ch * 8 + g8
                    rows = slice(16 * g8, 16 * g8 + 16)
                for g8 in range(8):
                    g = ch * 8 + g8
                    dv(lambda pt_=pt_, which=which, g=g, g8=g8: V.tensor_scalar(
                        s5w[:, which, g, :], pt_[:, 0:128], gmask[:, g8:g8 + 1], None, ALU.mult),
                        [pk, "gmask"], [("s5w", which, g)])
        for ch in range(2):
            dv(lambda ch=ch: V.tensor_copy(CC[:, 0, 0:64], cc_in[:, ch, 0, :]), ["cc_in"], ["CC"])
            dv(lambda ch=ch: V.tensor_scalar(CC[:, 0, 64:128], cc_in[:, ch, 1, :], -1.0, None, ALU.mult), ["cc_in"], ["CC"])
            dv(lambda ch=ch: V.tensor_scalar(CC[:, 1, 0:64], cc_in[:, ch, 1, :], -1.0, None, ALU.mult), ["cc_in"], ["CC"])
            dv(lambda ch=ch: V.tensor_scalar(CC[:, 1, 64:128], cc_in[:, ch, 0, :], -1.0, None, ALU.mult), ["cc_in"], ["CC"])
            for which in range(2):
                pt_, pk = bank("pM")
                t.op("pe", lambda pt_=pt_, which=which: PE_.transpose(pt_[:, 0:128], CC[:, which, :], idn[:]),
                     reads=["CC", "idn"], writes=[pk])
                for g8 in range(8):
                    g = ch * 8 + g8
                    dv(lambda pt_=pt_, which=which, g=g, g8=g8: V.tensor_copy(
                        s5w[:, 2 + which, g, 16 * g8:16 * g8 + 16], pt_[:, 16 * g8:16 * g8 + 16]), [pk], [("s5w", 2 + which, g)])
        bigu = ycat[:, 0:4, :].rearrange("p c t -> p (c t)")
        bigk = xT[:, 0:4, :].rearrange("p c t -> p (c t)")
        bigi = ycat[:, 4:8, :].rearrange("p c t -> p (c t)").bitcast(I32)
        KU = [("ycat", c) for c in range(4)]
        KK = [("xT", c) for c in range(4)]
        KI = [("ycat", c) for c in range(4, 8)]
        bu3 = bigu.rearrange("p (g j) -> p g j", g=16)
        dv(lambda: V.tensor_tensor(bu3, iot[:, :].unsqueeze(1).to_broadcast([128, 16, TS]),
                                   tmpB[:, :].unsqueeze(2).to_broadcast([128, 16, TS]), ALU.mult), ["iot", "tmpB"], KU)
        for which, shift in [(1, 0.0), (0, 0.25)]:
            dv(lambda shift=shift: V.tensor_scalar(bigk, bigu, shift, None, ALU.add), KU, KK)
            dv(lambda: V.tensor_copy(bigi, bigk), KK, KI)
            dv(lambda: V.tensor_copy(s5t[:, which, :, :].rearrange("p g j -> p (g j)"), bigi), KI, ["s5t"])
            dv(lambda which=which: V.tensor_tensor(bigk, bigk, s5t[:, which, :, :].rearrange("p g j -> p (g j)"), ALU.subtract),
               KK + ["s5t"], KK)
            ac(lambda which=which: A_.activation(s5t[:, which, :, :].rearrange("p g j -> p (g j)"), bigk, AF.Sin, scale=TWO_PI),
               KK, ["s5t"])
        t.dma("sp", s5scr_t[l], s5t[:, :, :, :].rearrange("p a g j -> p (a g j)"), reads=["s5t"], writes=[("scrt", l)], sem=st)
        t.dma("sp", s5scr_w[l], s5w[:, :, :, :].rearrange("p a g j -> p (a g j)"), reads=["s5w"] + [("s5w", a_, g_) for a_ in range(4) for g_ in range(16)], writes=[("scrw", l)], sem=st)

    gmask = sb("gmask", [128, 8])
    gm_t = sb("gm_t", [128, 8])
    t.op("pool", lambda: P_.iota(gm_t[:], pattern=[[16, 8]], base=0, channel_multiplier=-1,
                                 allow_small_or_imprecise_dtypes=True), writes=["gm_t"])
    t.op("dve", lambda: V.tensor_single_scalar(gmask[:], gm_t[:], 0.5, ALU.is_le), reads=["gm_t"], writes=["gmask"])
    t.op("dve", lambda: V.tensor_single_scalar(gm_t[:], gm_t[:], -15.5, ALU.is_ge), reads=["gm_t", "gmask"], writes=["gm_t"])
    t.op("dve", lambda: V.tensor_tensor(gmask[:], gmask[:], gm_t[:], ALU.mult), reads=["gm_t", "gmask"], writes=["gmask"])

    for l in range(2):
        build_s5(l)
        if stop == "const" and l == 0:
            dump(s5t[:, :, :, :].rearrange("p a g j -> p (a g j)"), ["s5t"], 4096)
            dump(rdec[:, 0, :], [("rdec", 0)], 16)
            dump(kre[:], ["kre"], 16)
            dump(kim[:], ["kim"], 16)
            dump(esink[:, 0, :, :].rearrange("p h c -> p (h c)"), ["esink"], 512)
            dump(c1[:, :, :].rearrange("p l c -> p (l c)"), ["c1"], 4)
            dump(gmask[:], ["gmask"], 8)
            dump(gN[:, :, :, :].rearrange("p l k c -> p (l k c)"), ["gN"], 64)
    if stop == "const":
        return finish()
    ftoks = []
    for k in ["bre2", "bim2", ("BB", 0), ("BB", 1), "BBt", "cc_in", "CC"]:
        if k in t.lastw:
            ftoks.append(t.lastw[k])
        ftoks.extend(t.readers.get(k, []))
    for j in range(NFF):
        t.readers.setdefault(("act", j), []).extend(ftoks)

    def load_s5(l):
        t.dma("sp", s5t[:, :, :, :].rearrange("p a g j -> p (a g j)"), s5scr_t[l], reads=[("scrt", l)], writes=["s5t"], sem=ld)
        t.dma("sp", s5w[:, :, :, :].rearrange("p a g j -> p (a g j)"), s5scr_w[l], reads=[("scrw", l)], writes=["s5w"], sem=ld)

    S5R = ["s5t"]
    S5W = ["s5w"]

    def win_cols(c):
        if c < 2:
            return [(0, 128 * c, 128)]
        if c < 6:
            return [(0, OFF_Q + 128 * (c - 2), 128)]
        if c < 8:
            h = c - 6
            return [(0, OFF_K + 64 * h, 64), (64, OFF_K + 64 * h, 64)]
        if c < 10:
            return [(0, OFF_LX + 128 * (c - 8), 128)]
        return [(0, OFF_LG + 128 * (c - 10), 128)]

    items = []
    NIDENT = 2 * (NFF + 8 + 12 + 3 + 8 + NFF + 8)
    wscr = nc.dram_tensor("wscr", [NIDENT, 128, RSZ], BF16, kind="Internal").ap()
    seen = {}
    USED = {"gu": 2048, "dn": NFF * 128, "win": 1024, "wvv": 2048, "wkv": 2048, "wxl": 2048, "wout": 1024}

    def emit_item(it, slot, key, sem):
        kind = it[0]
        used = USED[kind]
        if it in seen:
            idx = seen[it]
            t.dma("pool", slot[:, 0:used], wscr[idx, :, 0:used], reads=[("wscr", idx)], writes=[key], sem=sem)
            return
        emit_item_f32(it, slot, key, sem)
        idx = len(seen)
        seen[it] = idx
        t.dma("sp", wscr[idx, :, 0:used], slot[:, 0:used], reads=[key], writes=[("wscr", idx)], sem=st)

    def emit_item_f32(it, slot, key, sem):
        kind = it[0]

        def wd(dst, src):
            t.dma("pool", dst, src, writes=[key], sem=sem)
        if kind == "gu":
            _, l, f, j = it
            wg, wu = I[f"ffn{f}_w_gate"][l], I[f"ffn{f}_w_up"][l]
            wd(slot[:, 0:1024].rearrange("p (k f) -> p k f", k=8), wg[:, j * 128:(j + 1) * 128].rearrange("(k p) f -> p k f", p=128))
            wd(slot[:, 1024:2048].rearrange("p (k f) -> p k f", k=8), wu[:, j * 128:(j + 1) * 128].rearrange("(k p) f -> p k f", p=128))
        elif kind == "dn":
            _, l, f, m = it
            w_ = I[f"ffn{f}_w_down"][l]
            wd(slot[:, 0:NFF * 128].rearrange("p (j f) -> p j f", j=NFF), w_[:, m * 128:(m + 1) * 128].rearrange("(j p) f -> p j f", p=128))
        elif kind == "win":
            _, l, c = it
            for (dc, sc, wdt) in win_cols(c):
                wd(slot[:, 0:1024].rearrange("p (k f) -> p k f", k=8)[:, :, dc:dc + wdt],
                   I["w_in"][l][:, sc:sc + wdt].rearrange("(k p) f -> p k f", p=128))
        elif kind == "wvv":
            _, l = it
            for q_, h in enumerate([0, 0, 1, 1]):
                wd(slot[:, 0:2048].rearrange("p (k f) -> p k f", k=8)[:, :, 64 * q_:64 * q_ + 64],
                   I["w_in"][l][:, OFF_V + 64 * h:OFF_V + 64 * h + 64].rearrange("(k p) f -> p k f", p=128))
        elif kind == "wkv":
            _, l = it
            wd(slot[:, 0:2048].rearrange("p (k f) -> p k f", k=8), I["w_in"][l][:, OFF_K:OFF_K + 256].rearrange("(k p) f -> p k f", p=128))
        elif kind == "wxl":
            _, l = it
            wd(slot[:, 0:2048].rearrange("p (k f) -> p k f", k=8), I["w_in"][l][:, OFF_LX:OFF_LX + 256].rearrange("(k p) f -> p k f", p=128))
        elif kind == "wout":
            _, l, m = it
            wd(slot[:, 0:1024].rearrange("p (k f) -> p k f", k=8), I["w_out"][l][:, m * 128:(m + 1) * 128].rearrange("(k p) f -> p k f", p=128))

    tiles = [("P", i) for i in range(n_ptiles)]
    if do_sample:
        tiles.insert(min(1, len(tiles)), ("S", 0))

    def tile_needs_kvx(tile):
        return tile[0] == "S" or tile[1] == NPT - 1

    for tile in tiles:
        for l in range(2):
            for j in range(NFF):
                items.append(("gu", l, 1, j))
            for m in range(8):
                items.append(("dn", l, 1, m))
            for c in range(12):
                items.append(("win", l, c))
            items.append(("wvv", l))
            if tile_needs_kvx(tile):
                items.append(("wkv", l))
                items.append(("wxl", l))
            for m in range(8):
                items.append(("wout", l, m))
            for j in range(NFF):
                items.append(("gu", l, 2, j))
            for m in range(8):
                items.append(("dn", l, 2, m))
    wstate = {"next_emit": 0, "next_use": 0}

    def wnext(kind):
        idx = wstate["next_use"]
        assert items[idx][0] == kind, (items[idx], kind)
        lim = min(len(items), idx + R - 1)
        while wstate["next_emit"] < lim:
            i = wstate["next_emit"]
            k = i % R
            emit_item(items[i], ring[k], ("ring", k), ring_sem[k])
            wstate["next_emit"] += 1
        wstate["next_use"] += 1
        return ring[idx % R], ("ring", idx % R)

    XK = [("xT", c) for c in range(8)]
    HK = [("hT", c) for c in range(8)]
    ACTK = [("act", j) for j in range(NFF)]

    def fence(src_keys, dst_keys):
        toks = []
        for k in src_keys:
            if k in t.lastw:
                toks.append(t.lastw[k])
            toks.extend(t.readers.get(k, []))
        for k in dst_keys:
            t.readers.setdefault(k, []).extend(toks)

    def stat_acc(m, NT):
        sq = sqb[m % 2]
        sk = ("sqb", m % 2)
        t.op("act", lambda: A_.activation(sq[:, 0:NT], xT[:, m, 0:NT], AF.Square), reads=[("xT", m)], writes=[sk])
        todo = [m - 1] if m >= 1 else []
        if m == 7:
            todo.append(7)
        for mm2 in todo:
            mm_(PS["pA0"][:, 0:NT], ones_m[:, 0, :], sqb[mm2 % 2][:, 0:NT], mm2 == 0, mm2 == 7, [("sqb", mm2 % 2), ("ones_m", 0)], ["pA0"], True)

    def rmsnorm(src, srck, gains, NT, ngrp=None, pre=False):
        groups = ngrp or [(list(range(8)), 0)]
        for (chs, oi) in groups:
            if pre:
                pm, pk = PS["pA0"], "pA0"
            else:
                pm, pk = bank("pM")
            for n_, c in enumerate(chs):
                if pre:
                    break
                sq = sqb[n_ % 2]
                sk = ("sqb", n_ % 2)
                t.op("act", lambda sq=sq, c=c: A_.activation(sq[:, 0:NT], src[:, c, 0:NT], AF.Square), reads=[srck(c)], writes=[sk])
                mm_(pm[:, 0:NT], ones_m[:, oi, :], sq[:, 0:NT], n_ == 0, n_ == len(chs) - 1, [sk, ("ones_m", oi)], [pk], True)
            t.op("act", lambda pm=pm: A_.activation(rs1[:, 0:NT], pm[:, 0:NT], AF.Ln, bias=epsb[:, 0:1], scale=1.0), reads=[pk, "epsb"], writes=["rstd"])
            t.op("act", lambda: A_.activation(rstd[:, 0:NT], rs1[:, 0:NT], AF.Exp, scale=-0.5), reads=["rstd"], writes=["rstd"])
            for c in chs:
                t.op("dve", lambda c=c: V.scalar_tensor_tensor(hT[:, c, 0:NT], src[:, c, 0:NT], gains[:, c:c + 1], rstd[:, 0:NT], ALU.mult, ALU.mult),
                     reads=[srck(c), "rstd", "gN", "gF"], writes=[("hT", c)])

    epsb = sb("epsb", [128, 1])
    t.op("dve", lambda: V.memset(epsb[:], EPS), writes=["epsb"])

    def ffn(l, f, NT, pre=False):
        fence(["s5z", "s5w_", "s5tmp"], ACTK)
        fence(["s5A", "s5B"], HK)
        rmsnorm(xT, lambda c: ("xT", c), gN[:, l, 0 if f == 1 else 3, :], NT, pre=pre)
        for j in range(NFF):
            slot, sk = wnext("gu")
            pa, pak = bank("pA")
            pb, pbk = bank("pB")
            wg = slot[:, 0:1024].rearrange("p (k f) -> p k f", k=8)
            wu = slot[:, 1024:2048].rearrange("p (k f) -> p k f", k=8)
            for k in range(8):
                mm_(pa[:, 0:NT], wg[:, k, :], hT[:, k, 0:NT], k == 0, k == 7, [sk, ("hT", k)], [pak], k == 7)
            for k in range(8):
                mm_(pb[:, 0:NT], wu[:, k, :], hT[:, k, 0:NT], k == 0, k == 7, [sk, ("hT", k)], [pbk], k == 7)
            s_ = sg[j % 2]
            t.op("act", lambda pa=pa, s_=s_: A_.activation(s_[:, 0:NT], pa[:, 0:NT], AF.Silu), reads=[pak], writes=[("sg", j % 2)])
            t.op("dve", lambda pb=pb, s_=s_, j=j: V.tensor_tensor(act[:, j, 0:NT], s_[:, 0:NT], pb[:, 0:NT], ALU.mult),
                 reads=[("sg", j % 2), pbk], writes=[("act", j)])
        for m in range(8):
            slot, sk = wnext("dn")
            wd_ = slot[:, 0:NFF * 128].rearrange("p (j f) -> p j f", j=NFF)
            po, pok = bank("pO")
            for j in range(NFF):
                mm_(po[:, 0:NT], wd_[:, j, :], act[:, j, 0:NT], j == 0, j == NFF - 1, [sk, ("act", j)], [pok], j == NFF - 1)
            t.op("dve", lambda po=po, m=m: V.scalar_tensor_tensor(xT[:, m, 0:NT], po[:, 0:NT], 0.5, xT[:, m, 0:NT], ALU.mult, ALU.add),
                 reads=[pok, ("xT", m)], writes=[("xT", m)])
            stat_acc(m, NT)

    def load_x(src_rows, NT):
        fence(["s5dec"], [("xtm", 0), ("xtm", 1)])
        nb = (NT + 127) // 128
        for b in range(nb):
            rows = min(128, NT - b * 128)
            xb = xtm[b % 2]
            t.dma("sp", xb[0:rows, :], src_rows[b * 128:b * 128 + rows, :], writes=[("xtm", b % 2)], sem=ld)
            for half in range(2):
                pm, pk = bank("pM")
                for c4 in range(4):
                    c = half * 4 + c4
                    t.op("pe", lambda pm=pm, c4=c4, c=c, xb=xb, rows=rows: PE_.transpose(
                        pm[:, c4 * 128:c4 * 128 + rows], xb[0:rows, c * 128:(c + 1) * 128], idn[0:rows, 0:rows]),
                        reads=[("xtm", b % 2), "idn"], writes=[pk], signal=(c4 == 3))
                eng = "act" if half == 0 else "dve"
                outap = xT[:, half * 4:half * 4 + 4, b * 128:b * 128 + rows]
                inap = pm[:, :].rearrange("p (c t) -> p c t", c=4)[:, :, 0:rows]
                if eng == "act":
                    t.op("act", lambda outap=outap, inap=inap: A_.copy(outap, inap), reads=[pk], writes=[("xT", half * 4 + i) for i in range(4)])
                else:
                    t.op("dve", lambda outap=outap, inap=inap: V.tensor_copy(outap, inap), reads=[pk], writes=[("xT", half * 4 + i) for i in range(4)])

    def store_y(dst_rows, NT):
        YK = [("ycat", c) for c in range(8)]
        pm, pk = PS["pA0"], "pA0"
        t.op("act", lambda pm=pm: A_.activation(rs1[:, 0:NT], pm[:, 0:NT], AF.Ln, bias=epsb[:, 0:1], scale=1.0), reads=[pk, "epsb"], writes=["rstd"])
        t.op("act", lambda: A_.activation(rstd[:, 0:NT], rs1[:, 0:NT], AF.Exp, scale=-0.5), reads=["rstd"], writes=["rstd"])
        for c in range(8):
            t.op("dve", lambda c=c: V.scalar_tensor_tensor(ycat[:, c, 0:NT], xT[:, c, 0:NT], gF[:, c:c + 1], rstd[:, 0:NT], ALU.mult, ALU.mult),
                 reads=[("xT", c), "rstd", "gF"], writes=[("ycat", c)])
        nb = (NT + 127) // 128
        fence(["s5dec"], [("xtm", 0), ("xtm", 1)])
        for b in range(nb):
            rows = min(128, NT - b * 128)
            xb = xtm[b % 2]
            for half in range(2):
                pm, pk = bank("pM")
                for c4 in range(4):
                    c = half * 4 + c4
                    t.op("pe", lambda pm=pm, c4=c4, c=c, rows=rows, b=b: PE_.transpose(
                        pm[0:rows, c4 * 128:(c4 + 1) * 128], ycat[:, c, b * 128:b * 128 + rows], idn[:]),
                        reads=[("ycat", c), "idn"], writes=[pk], signal=(c4 == 3))
                if half == 0:
                    t.op("act", lambda pm=pm, xb=xb, rows=rows: A_.copy(xb[0:rows, 0:512], pm[0:rows, :]), reads=[pk], writes=[("xtm", b % 2)])
                else:
                    t.op("dve", lambda pm=pm, xb=xb, rows=rows: V.tensor_copy(xb[0:rows, 512:1024], pm[0:rows, :]), reads=[pk], writes=[("xtm", b % 2)])
            t.dma("sp", dst_rows[b * 128:b * 128 + rows, :], xb[0:rows, :], reads=[("xtm", b % 2)], sem=st)

    def mixer(l, tile):
        kind, ti = tile
        isP = kind == "P"
        NT = TT if isP else NS * LS
        nseq = 1 if isP else NS
        L = NT // nseq
        rmsnorm(xT, lambda c: ("xT", c), gN[:, l, 1, :], NT, pre=True)
        xl3 = lambda c: xlT[:, c, 0:nseq * (3 + L)].rearrange("p (s t) -> p s t", s=nseq)
        if isP:
            for c in range(2):
                t.op("dve", lambda c=c: V.tensor_copy(xlT[:, c, 0:3], convh_p[:, l, c, :]), reads=[("convh_p", l)], writes=[("xlT", c)])
            if ti > 0:
                t.op("dve", lambda: V.tensor_copy(kkT[:, :, 0:128], kk_halo[:, l, :, :]), reads=[("kk_halo", l)], writes=["kkT"])
                t.op("dve", lambda: V.tensor_copy(vtm[:, 0, :], v_halo[:, l, :]), reads=[("v_halo", l)], writes=["vtm"])
        else:
            for b in range(NS):
                for c in range(2):
                    ldc(xl3(c)[:, b, 0:3], I["sconv"][l, b][:, c * 128:(c + 1) * 128].rearrange("t p -> p t"), ("xlT", c), **NC_)
            ldc(lru_s[:, l, :, :], I["slru"][l].rearrange("b (c p) -> p b c", p=128), ("lru_s", l), **NC_)
            for b in range(NS):
                ldc(hs_s[0:64, l, b, :], I["sre"][l, b].rearrange("g p -> p g"), ("hs_s", l), **NC_)
                ldc(hs_s[64:128, l, b, :], I["sim"][l, b].rearrange("g p -> p g"), ("hs_s", l), **NC_)
        if stop == "m_pre":
            raise _Stop()
        def gen_proj():
            for c in range(12):
                if c > 0:
                    yield ("chunk", c)
                if stop is not None and stop.startswith("m_c") and c == int(stop[3:]):
                    raise _Stop()
                slot, sk = wnext("win")
                w_ = slot[:, 0:1024].rearrange("p (k f) -> p k f", k=8)
                po, pok = bank("pO")
                for k in range(8):
                    mm_(po[:, 0:NT], w_[:, k, :], hT[:, k, 0:NT], k == 0, k == 7, [sk, ("hT", k)], [pok], k == 7)
                if c < 2:
                    t.op("dve", lambda po=po, c=c: V.tensor_copy(uT[:, c, 0:NT], po[:, 0:NT]), reads=[pok], writes=[("uT", c)])
                    t.op("act", lambda c=c: A_.copy(ubf[:, c, 0:NT], uT[:, c, 0:NT]), reads=[("uT", c)], writes=[("ubf", c)])
                elif c < 6:
                    t.op("act", lambda po=po, c=c: A_.copy(qT[:, c - 2, 0:NT], po[:, 0:NT]), reads=[pok], writes=["qT"])
                elif c < 8:
                    t.op("act", lambda po=po, c=c: A_.copy(kkT[:, c - 6, 128:128 + NT], po[:, 0:NT]), reads=[pok], writes=["kkT"])
                elif c < 10:
                    cc = c - 8
                    t.op("act", lambda po=po, cc=cc: A_.copy(xl3(cc)[:, :, 3:3 + L], po[:, 0:NT].rearrange("p (s t) -> p s t", s=nseq)),
                         reads=[pok], writes=[("xlT", cc)])
                else:
                    cc = c - 10
                    t.op("act", lambda po=po, cc=cc: A_.activation(gg[:, cc, 0:NT], po[:, 0:NT], AF.Gelu_apprx_tanh), reads=[pok], writes=[("gg", cc)])
            yield "projdone"
            slot, sk = wnext("wvv")
            wv = slot[:, 0:2048].rearrange("p (k f) -> p k f", k=8)
            if isP:
                for b in range(NT // 128):
                    po, pok = bank("pO")
                    for k in range(8):
                        mm_(po[:, 0:256], hT[:, k, b * 128:(b + 1) * 128], wv[:, k, :], k == 0, k == 7, [sk, ("hT", k)], [pok], k == 7)
                    t.op("act", lambda po=po, b=b: A_.copy(vtm[:, 1 + b, :], po[:, 0:256]), reads=[pok], writes=["vtm"])
            else:
                for b in range(NS):
                    po, pok = bank("pO")
                    for k in range(8):
                        mm_(po[0:LS, 0:256], hT[:, k, b * LS:(b + 1) * LS], wv[:, k, :], k == 0, k == 7, [sk, ("hT", k)], [pok], k == 7)
                    t.op("act", lambda po=po, b=b: A_.copy(vnew[:, b, :], po[0:LS, 0:256]), reads=[pok], writes=["vnew"])
            if stop == "m_vv":
                raise _Stop()
            if tile_needs_kvx(tile):
                s1, sk1 = wnext("wkv")
                s2, sk2 = wnext("wxl")
                wkv = s1[:, 0:2048].rearrange("p (k f) -> p k f", k=8)
                wxl = s2[:, 0:2048].rearrange("p (k f) -> p k f", k=8)
                segs = [(NT - 128, 128, None)] if isP else [(b * LS, LS, b) for b in range(NS)]
                for (c0, n_, b) in segs:
                    po, pok = bank("pO")
                    for k in range(8):
                        mm_(po[0:n_, 0:256], hT[:, k, c0:c0 + n_], wkv[:, k, :], k == 0, k == 7, [sk1, ("hT", k)], [pok], k == 7)
                    for k in range(8):
                        mm_(po[0:n_, 256:512], hT[:, k, c0:c0 + n_], wxl[:, k, :], k == 0, k == 7, [sk2, ("hT", k)], [pok], k == 7)
                    t.op("dve", lambda po=po, n_=n_: V.tensor_copy(kvx[0:n_, :], po[0:n_, :]), reads=[pok], writes=["kvx"])
                    if isP:
                        t.dma("sp", O["kp"][l], kvx[:, 0:128], reads=["kvx"], sem=st)
                        t.dma("sp", O["vp"][l], kvx[:, 128:256], reads=["kvx"], sem=st)
                        t.dma("sp", O["convp"][l], kvx[125:128, 256:512], reads=["kvx"], sem=st)
                    else:
                        t.dma("sp", O["ks"][l, b], kvx[0:LS, 0:128], reads=["kvx"], sem=st)
                        t.dma("sp", O["vs"][l, b], kvx[0:LS, 128:256], reads=["kvx"], sem=st)
                        t.dma("sp", O["convs"][l, b], kvx[LS - 3:LS, 256:512], reads=["kvx"], sem=st)
            if stop == "m_proj":
                raise _Stop()
            yield "end"

        def gen_s5():
            segs = [(s * TS, TS, None) for s in range(NT // TS)] if isP else [(b * LS, LS, b) for b in range(NS)]
            n_ = TS if isP else LS
            fence(ACTK, ["s5z", "s5w_", "s5tmp"])
            fence(HK, ["s5A", "s5B"])
            fence([("xtm", 0), ("xtm", 1)], ["s5dec"])
            z3 = zbuf[:, 0:16 * n_].rearrange("p (g j) -> p g j", g=16)
            w3 = wbuf[:, 0:16 * n_].rearrange("p (g j) -> p g j", g=16)
            A3 = Abuf[:, 0:16 * n_].rearrange("p (g j) -> p g j", g=16)
            B3 = Bbuf[:, 0:16 * n_].rearrange("p (g j) -> p g j", g=16)
            d3 = decb[:, 0:16 * n_].rearrange("p (g j) -> p g j", g=16)
            t3 = tbuf[:, 0:8 * n_].rearrange("p (g j) -> p g j", g=8)
            t.op("dve", lambda: V.tensor_copy(d3, rdec[:, l, :].unsqueeze(2).to_broadcast([128, 16, n_])), reads=[("rdec", l)], writes=["s5dec"])
            t.op("dve", lambda: V.memset(d3[:, :, 0:1], 0.0), reads=[], writes=["s5dec"])
            if isP:
                t.op("dve", lambda: V.tensor_copy(hcur[:], hs_p[:, l, :]), reads=[("hs_p", l)], writes=["hcur"])
            XA = ["pA0", "pA1"]
            XB = ["pB0", "pB1"]
            fold32 = sb_once("fold32", [128, 16])
            nseg = len(segs)

            def Xmm(si, half):
                c0 = segs[si][0]
                for g8 in range(8):
                    g = half * 8 + g8
                    mm_(pX[:, g8 * 128:g8 * 128 + n_], s5w[:, 0, g, :], ubf[:, half, c0:c0 + n_], True, True, S5W + [("ubf", half)], XA, g8 == 7)
                for g8 in range(8):
                    g = half * 8 + g8
                    mm_(pX[:, 1024 + g8 * 128:1024 + g8 * 128 + n_], s5w[:, 1, g, :], ubf[:, half, c0:c0 + n_], True, True, S5W + [("ubf", half)], XB, g8 == 7)

            def Mod(si, half):
                x1 = pX[:, 0:1024].rearrange("p (g j) -> p g j", g=8)[:, :, 0:n_]
                x2 = pX[:, 1024:2048].rearrange("p (g j) -> p g j", g=8)[:, :, 0:n_]
                zh = z3[:, half * 8:half * 8 + 8, :]
                t.op("dve", lambda: V.tensor_tensor(zh, x1, s5t[:, 0, half * 8:half * 8 + 8, 0:n_], ALU.mult), reads=XA + S5R, writes=["s5z"])
                t.op("dve", lambda: V.tensor_tensor(t3, x2, s5t[:, 1, half * 8:half * 8 + 8, 0:n_], ALU.mult), reads=XB + S5R, writes=["s5tmp"])
                t.op("dve", lambda: V.tensor_tensor(zh, zh, t3, ALU.add), reads=["s5z", "s5tmp"], writes=["s5z"])

            def init_of(si):
                if not isP:
                    return hs_s[:, l, segs[si][2], :], ("hs_s", l)
                return hcur[:], "hcur"

            pys = {}
            Xmm(0, 0)
            Mod(0, 0)
            yield
            Xmm(0, 1)
            Mod(0, 1)
            yield
            for si, (c0, _n, b) in enumerate(segs):
                iap, ikey = init_of(si)
                t.op("dve", lambda: V.tensor_tensor(fold32[:], rdec[:, l, :], iap, ALU.mult), reads=[("rdec", l), ikey], writes=["fold32"])
                t.op("dve", lambda: V.tensor_tensor(z3[:, :, 0], z3[:, :, 0], fold32[:], ALU.add), reads=["s5z", "fold32"], writes=["s5z"])
                t.op("dve", lambda: V.tensor_tensor_scan(wbuf[:, 0:16 * n_], decb[:, 0:16 * n_], zbuf[:, 0:16 * n_], 0.0, ALU.mult, ALU.add),
                     reads=["s5z", "s5dec"], writes=["s5w_"])
                if si == 0:
                    fence(HK, ["s5A", "s5B"])
                t.op("dve", lambda: V.tensor_tensor(A3, w3, s5t[:, 0, :, 0:n_], ALU.mult), reads=["s5w_"] + S5R, writes=["s5A"])
                t.op("dve", lambda: V.tensor_tensor(B3, w3, s5t[:, 1, :, 0:n_], ALU.mult), reads=["s5w_"] + S5R, writes=["s5B"])
                t.op("dve", lambda: V.tensor_tensor(a32[:], w3[:, :, n_ - 1], s5t[:, 0, :, n_ - 1], ALU.mult), reads=["s5w_"] + S5R, writes=["a32"])
                t.op("dve", lambda: V.tensor_tensor(b32[:], w3[:, :, n_ - 1], s5t[:, 1, :, n_ - 1], ALU.mult), reads=["s5w_"] + S5R, writes=["b32"])
                yield
                if si + 1 < nseg:
                    Xmm(si + 1, 0)
                pyl = []
                for ch in range(2):
                    py, pyk = bank("pO")
                    pyl.append((py, pyk))
                    for g8 in range(8):
                        g = ch * 8 + g8
                        mm_(py[:, 0:n_], s5w[:, 2, g, :], A3[:, g, :], g8 == 0, False, S5W + ["s5A"], [pyk], False)
                        mm_(py[:, 0:n_], s5w[:, 3, g, :], B3[:, g, :], False, g8 == 7, S5W + ["s5B"], [pyk], g8 == 7)
                ph, phk = bank("pM")
                mm_(ph[:, 0:16], Pm[:], b32[:], True, True, ["Pm", "b32"], [phk], True)
                if si + 1 < nseg:
                    Mod(si + 1, 0)
                for ch in range(2):
                    py, pyk = pyl[ch]
                    t.op("dve", lambda py=py, ch=ch: V.scalar_tensor_tensor(ycat[:, ch, c0:c0 + n_], uT[:, ch, c0:c0 + n_], dD[:, l, ch:ch + 1],
                                                                             py[:, 0:n_], ALU.mult, ALU.add),
                         reads=[pyk, ("uT", ch), "dD"], writes=[("ycat", ch)])
                t.op("dve", lambda ph=ph: V.tensor_tensor(hcur[:], a32[:], ph[:, 0:16], ALU.add), reads=["a32", phk], writes=["hcur"])
                if not isP:
                    emit_state_out(hcur, "hcur", O["sres"][l, b], O["sims"][l, b])
                if si + 1 < nseg:
                    Xmm(si + 1, 1)
                    Mod(si + 1, 1)
                yield
            if isP:
                t.op("dve", lambda: V.tensor_copy(hs_p[:, l, :], hcur[:]), reads=["hcur"], writes=[("hs_p", l)])
                if ti == NPT - 1:
                    emit_state_out(hcur, "hcur", O["srep"][l], O["simp"][l])
            for ch in range(2):
                t.op("act", lambda ch=ch: A_.activation(ycat[:, ch, 0:NT], ycat[:, ch, 0:NT], AF.Gelu_apprx_tanh), reads=[("ycat", ch)], writes=[("ycat", ch)])
                t.op("dve", lambda ch=ch: V.tensor_copy(zb[:, ch, 0:NT], ycat[:, ch, 0:NT]), reads=[("ycat", ch)], writes=[("zb", ch)])
            for m in range(2):
                po, pok = bank("pO")
                for k in range(2):
                    mm_(po[:, 0:NT], wglu[:, l, k, m * 128:(m + 1) * 128], zb[:, k, 0:NT], k == 0, k == 1, ["wglu", ("zb", k)], [pok], k == 1)
                t.op("act", lambda po=po: A_.activation(sig[:, 0:NT], po[:, 0:NT], AF.Sigmoid), reads=[pok], writes=["sig"])
                t.op("dve", lambda m=m: V.tensor_tensor(ycat[:, m, 0:NT], ycat[:, m, 0:NT], sig[:, 0:NT], ALU.mult), reads=["sig", ("ycat", m)], writes=[("ycat", m)])
            yield

        def gen_attn():
            if not isP:
                cur_b = [None]
            nq_chunks = NT // 64 if isP else NS
            for qc in range(nq_chunks):
                if isP:
                    q0, nq = 64 * qc, 64
                    blocks = []
                    c = qc
                    if c % 2 == 0:
                        fb, hb, hrows = c // 2, c // 2 + 1, slice(0, 64)
                    else:
                        hb, hrows, fb = (c - 1) // 2, slice(64, 128), (c + 1) // 2
                    if ti == 0 and c == 0:
                        blocks = [(hb, hrows)]
                    elif ti == 0 and c == 1:
                        blocks = [(fb, slice(0, 128))]
                    else:
                        blocks = [(fb, slice(0, 128)), (hb, hrows)]
                else:
                    b = qc
                    q0, nq = LS * b, LS
                    ldc(ckf[:], I["ck"][l, b], "ckf")
                    ldc(cvf[:], I["cv"][l, b], "cvf")
                    for h in range(2):
                        for d2 in range(2):
                            t.op("dve", lambda h=h, d2=d2: V.tensor_copy(ckd[:, h, d2 * 64:(d2 + 1) * 64], ckf[:, h * 64:(h + 1) * 64]), reads=["ckf"], writes=["ckd"])
                            t.op("act", lambda h=h, d2=d2: A_.copy(vtm[:, 0, (2 * h + d2) * 64:(2 * h + d2 + 1) * 64], cvf[:, h * 64:(h + 1) * 64]), reads=["cvf"], writes=["vtm"])
                    for h in range(2):
                        ptk, ptkk = bank("pM")
                        t.op("pe", lambda ptk=ptk, h=h: PE_.transpose(ptk[:, 0:128], ckd[:, h, :], idn[:]), reads=["ckd", "idn"], writes=[ptkk])
                        t.op("dve", lambda ptk=ptk, h=h: V.tensor_copy(ckT[:, h, :], ptk[:, 0:128]), reads=[ptkk], writes=["ckT"])
                    blocks = [("cache", slice(0, 128)), ("new", slice(0, LS))]
                    if stop == "m_a1":
                        raise _Stop()
                nqc = 2 * nq
                for h in range(2):
                    pscs = [(PS["pM0"], "pM0"), (PS["pM1"], "pM1")]
                    pb_ = pT[h % 2]
                    pbk = ("pT", h % 2)
                    for bi, (blk, rows) in enumerate(blocks):
                        for gp in range(2):
                            psc, psk = pscs[gp]
                            r_ = slice(gp * 64, gp * 64 + 64)
                            if blk == "cache":
                                lk, lkk, M_ = ckT[r_, h, :], "ckT", 128
                            elif blk == "new":
                                lk, lkk, M_ = kkT[r_, h, 128 + q0:128 + q0 + LS], "kkT", LS
                            else:
                                lk, lkk, M_ = kkT[r_, h, blk * 128:(blk + 1) * 128], "kkT", 128
                            for i_ in range(2):
                                mm_(psc[0:M_, bi * 128 + i_ * nq: bi * 128 + (i_ + 1) * nq], lk, qT[r_, 2 * h + i_, q0:q0 + nq], True, True,
                                    [lkk, "qT"], [psk], (bi == len(blocks) - 1 and i_ == 1))
                    if stop == "m_a2":
                        raise _Stop()
                    for bi, (blk, rows) in enumerate(blocks):
                        for gp in range(2):
                            psc, psk = pscs[gp]
                            t.op("act", lambda psc=psc, pb_=pb_, bi=bi, rows=rows, gp=gp: A_.activation(
                                pb_[rows, bi, gp * nqc:(gp + 1) * nqc], psc[rows, bi * 128:bi * 128 + nqc], AF.Exp, scale=0.125),
                                reads=[psk], writes=[pbk])
                    if stop == "m_a3":
                        raise _Stop()
                    po, pok = bank("pO")
                    for bi, (blk, rows) in enumerate(blocks):
                        if blk == "cache":
                            lv, lvk = vtm[rows, 0, h * 128:(h + 1) * 128], "vtm"
                        elif blk == "new":
                            lv, lvk = vnew[rows, b, h * 128:(h + 1) * 128], "vnew"
                        else:
                            lv, lvk = vtm[rows, blk, h * 128:(h + 1) * 128], "vtm"
                        mm_(po[:, 0:2 * nqc], lv, pb_[rows, bi, 0:2 * nqc], bi == 0, bi == len(blocks) - 1, [lvk, pbk], [pok], bi == len(blocks) - 1)
                    pd_, pdk = bank("pB")
                    for bi, (blk, rows) in enumerate(blocks):
                        mm_(pd_[:, 0:2 * nqc], ones_b[rows, :], pb_[rows, bi, 0:2 * nqc], bi == 0, False, ["ones_b", pbk], [pdk], False)
                    esr = esbP[0:1, l, h, :] if isP else esbS[0:1, l, h, :]
                    mm_(pd_[:, 0:2 * nqc], ones_b[0:1, :], esr, False, True, ["ones_b", "esb"], [pdk], True)
                    if stop == "m_a4":
                        raise _Stop()
                    dsb = dn_sb[h % 2]
                    dk = ("dnsb", h % 2)
                    t.op("act", lambda pd_=pd_, dsb=dsb: A_.activation(dsb[:, 0:2 * nqc], pd_[:, 0:2 * nqc], AF.Ln), reads=[pdk], writes=[dk])
                    t.op("act", lambda dsb=dsb: A_.activation(dsb[:, 0:2 * nqc], dsb[:, 0:2 * nqc], AF.Exp, scale=-1.0), reads=[dk], writes=[dk])
                    for gp in range(2):
                        r_ = slice(gp * 64, gp * 64 + 64)
                        cs = slice(gp * nqc, (gp + 1) * nqc)
                        t.op("dve", lambda po=po, r_=r_, dsb=dsb, cs=cs, h=h: V.tensor_tensor(
                            ycat[r_, 2 + 2 * h:4 + 2 * h, q0:q0 + nq], po[r_, cs].rearrange("p (i t) -> p i t", i=2),
                            dsb[r_, cs].rearrange("p (i t) -> p i t", i=2), ALU.mult),
                            reads=[pok, dk], writes=[("ycat", 2 + 2 * h), ("ycat", 3 + 2 * h)])
                    yield
            if isP:
                t.op("dve", lambda: V.tensor_copy(kk_halo[:, l, :, :], kkT[:, :, NT:NT + 128]), reads=["kkT"], writes=[("kk_halo", l)])
                t.op("dve", lambda: V.tensor_copy(v_halo[:, l, :], vtm[:, NT // 128, :]), reads=["vtm"], writes=[("v_halo", l)])
            yield

        def gen_lru():
            for c in range(2):
                x3 = xl3(c)
                xc3 = xc[:, 0:NT].rearrange("p (s t) -> p s t", s=nseq)
                t.op("dve", lambda x3=x3, c=c: V.tensor_scalar(xc3, x3[:, :, 0:L], cw[:, l, 0, c:c + 1], cb[:, l, c:c + 1], ALU.mult, ALU.add),
                     reads=[("xlT", c), "cw", "cb"], writes=["xc"])
                for tt_ in range(1, 4):
                    t.op("dve", lambda x3=x3, c=c, tt_=tt_: V.scalar_tensor_tensor(xc3, x3[:, :, tt_:tt_ + L], cw[:, l, tt_, c:c + 1], xc3, ALU.mult, ALU.add),
                         reads=[("xlT", c), "cw", "xc"], writes=["xc"])
                if isP:
                    t.op("dve", lambda c=c: V.tensor_copy(convh_p[:, l, c, :], xlT[:, c, L:L + 3]), reads=[("xlT", c)], writes=[("convh_p", l)])
                t.op("act", lambda: A_.copy(xcb[:, 0:NT], xc[:, 0:NT]), reads=["xc"], writes=["xcb"])
                yield
                pa, pak = bank("pA")
                pb, pbk2 = bank("pB")
                mm_(pa[:, 0:NT], wlru[:, l, c, :], xcb[:, 0:NT], True, True, ["wlru", "xcb"], [pak], True)
                mm_(pb[:, 0:NT], wlru[:, l, 2 + c, :], xcb[:, 0:NT], True, True, ["wlru", "xcb"], [pbk2], True)
                t.op("act", lambda pa=pa, c=c: A_.activation(rr[:, 0:NT], pa[:, 0:NT], AF.Sigmoid, bias=ba[:, l, c:c + 1], scale=1.0), reads=[pak, "ba"], writes=["rr"])
                t.op("act", lambda pb=pb, c=c: A_.activation(ii[:, 0:NT], pb[:, 0:NT], AF.Sigmoid, bias=bx[:, l, c:c + 1], scale=1.0), reads=[pbk2, "bx"], writes=["ii"])
                t.op("act", lambda c=c: A_.activation(aa[:, 0:NT], rr[:, 0:NT], AF.Exp, scale=c1[:, l, c:c + 1]), reads=["rr", "c1"], writes=["aa"])
                t.op("act", lambda c=c: A_.activation(mm[:, 0:NT], rr[:, 0:NT], AF.Exp, scale=c2[:, l, c:c + 1]), reads=["rr", "c2"], writes=["mm"])
                t.op("act", lambda: A_.activation(mm[:, 0:NT], mm[:, 0:NT], AF.Sqrt, bias=oneb[:, 0:1], scale=-1.0), reads=["mm", "oneb"], writes=["mm"])
                yield
                t.op("dve", lambda: V.tensor_tensor(mm[:, 0:NT], mm[:, 0:NT], ii[:, 0:NT], ALU.mult), reads=["mm", "ii"], writes=["mm"])
                t.op("dve", lambda: V.tensor_tensor(mm[:, 0:NT], mm[:, 0:NT], xc[:, 0:NT], ALU.mult), reads=["mm", "xc"], writes=["mm"])
                for s_ in range(nseq):
                    cs = slice(s_ * L, (s_ + 1) * L)
                    init = lru_p[:, l, c:c + 1] if isP else lru_s[:, l, s_, c:c + 1]
                    ik = ("lru_p", l) if isP else ("lru_s", l)
                    t.op("dve", lambda cs=cs, init=init, c=c: V.tensor_tensor_scan(ycat[:, 6 + c, cs], aa[:, cs], mm[:, cs], init, ALU.mult, ALU.add),
                         reads=["aa", "mm", ik], writes=[("ycat", 6 + c)])
                    if isP:
                        t.op("dve", lambda c=c: V.tensor_copy(lru_p[:, l, c:c + 1], ycat[:, 6 + c, L - 1:L]), reads=[("ycat", 6 + c)], writes=[("lru_p", l)])
                    else:
                        t.op("dve", lambda c=c, s_=s_: V.tensor_copy(lru_s[:, l, s_, c:c + 1], ycat[:, 6 + c, (s_ + 1) * L - 1:(s_ + 1) * L]),
                             reads=[("ycat", 6 + c)], writes=[("lru_s", l)])
                t.op("dve", lambda c=c: V.tensor_tensor(ycat[:, 6 + c, 0:NT], ycat[:, 6 + c, 0:NT], gg[:, c, 0:NT], ALU.mult),
                     reads=[("ycat", 6 + c), ("gg", c)], writes=[("ycat", 6 + c)])
            if isP and ti == NPT - 1:
                t.dma("sp", O["lrup"][l].rearrange("(c p) -> p c", p=128), lru_p[:, l, :], reads=[("lru_p", l)], sem=st, **NC_)
            if not isP:
                t.dma("sp", O["lrus"][l].rearrange("b (c p) -> p b c", p=128), lru_s[:, l, :, :], reads=[("lru_s", l)], sem=st, **NC_)
            yield

        g_proj = gen_proj()
        g_s5 = gen_s5()

        def adv_proj(until):
            while True:
                try:
                    r_ = next(g_proj)
                except StopIteration:
                    return
                if r_ == until:
                    return
        adv_proj(("chunk", 2))
        next(g_s5)
        adv_proj(("chunk", 6))
        next(g_s5)
        adv_proj(None)
        gens = [g_s5, gen_attn(), gen_lru()]
        while gens:
            for g_ in list(gens):
                try:
                    next(g_)
                except StopIteration:
                    gens.remove(g_)

        fence(["s5A", "s5B"], HK)
        rmsnorm(ycat, lambda c: ("ycat", c), gN[:, l, 2, :], NT, ngrp=[([0, 1], 2), ([2, 3, 4, 5], 1), ([6, 7], 2)])
        for m in range(8):
            slot, sk = wnext("wout")
            w_ = slot[:, 0:1024].rearrange("p (k f) -> p k f", k=8)
            po, pok = bank("pO")
            for k in range(8):
                mm_(po[:, 0:NT], w_[:, k, :], hT[:, k, 0:NT], k == 0, k == 7, [sk, ("hT", k)], [pok], k == 7)
            t.op("dve", lambda po=po, m=m: V.tensor_tensor(xT[:, m, 0:NT], xT[:, m, 0:NT], po[:, 0:NT], ALU.add), reads=[pok, ("xT", m)], writes=[("xT", m)])
            stat_acc(m, NT)

    oneb = sb("oneb", [128, 1])
    t.op("dve", lambda: V.memset(oneb[:], 1.0), writes=["oneb"])

    def emit_state_out(src, srckey, dre, dim):
        t.dma("sp", dre.rearrange("g p -> p g"), src[0:64, :], reads=[srckey], sem=st, **NC_)
        t.dma("sp", dim.rearrange("g p -> p g"), src[64:128, :], reads=[srckey], sem=st, **NC_)

    for tile in tiles:
        kind, ti = tile
        if kind == "P":
            NT = TT
            load_x(I["xp"][ti * TT:(ti + 1) * TT, :], NT)
        else:
            NT = NS * LS
            load_x(I["xs"], NT)
        if stop == "load_x":
            dump(xT[:, :, 0:64], XK, 512)
            return finish()
        for l in range(2):
            load_s5(l)
            ffn(l, 1, NT, pre=(l > 0))
            if stop == "ffn1":
                dump(xT[:, :, 0:64], XK, 512)
                return finish()
            try:
                mixer(l, tile)
            except _Stop:
                dump(xT[:, :, 0:64], XK, 512)
                dump(ycat[:, :, 0:64], [("ycat", c) for c in range(8)], 512)
                dump(uT[:, :, 0:64], [("uT", 0), ("uT", 1)], 128)
                dump(gg[:, :, 0:64], [("gg", 0), ("gg", 1)], 128)
                dump(xlT[:, :, 0:76], [("xlT", 0), ("xlT", 1)], 152)
                return finish()
            if stop == "mixer":
                dump(xT[:, :, 0:64], XK, 512)
                dump(ycat[:, :, 0:64], [("ycat", c) for c in range(8)], 512)
                return finish()
            ffn(l, 2, NT, pre=True)
        if kind == "P":
            store_y(O["yp"][ti * TT:(ti + 1) * TT, :], NT)
        else:
            store_y(O["ys"], NT)
    return finish()


_CACHE = {}


def kernel(**inp):
    f = lambda a: np.ascontiguousarray(np.asarray(a, dtype=np.float32))
    if "nc" not in _CACHE:
        _CACHE["nc"] = build_program()
    nc = _CACHE["nc"]
    W = {n: f(inp[n]) for n in WNAMES}
    xp, xs = f(inp["x_prompt"]), f(inp["x_sample"])
    ck, cv = f(inp["cache_k"]), f(inp["cache_v"])
    sre, sim = f(inp["state_ssm_re"]), f(inp["state_ssm_im"])
    sconv, slru = f(inp["state_conv"]), f(inp["state_lru"])
    in_maps = []
    for c in range(8):
        sl = slice(NS * c, NS * c + NS)
        m = {"xp": xp[c % 4], "xs": np.ascontiguousarray(xs[sl].reshape(NS * LS, D)),
             "ck": np.ascontiguousarray(ck[:, sl].reshape(2, NS, 128, 128)),
             "cv": np.ascontiguousarray(cv[:, sl].reshape(2, NS, 128, 128)),
             "sre": np.ascontiguousarray(sre[:, sl]), "sim": np.ascontiguousarray(sim[:, sl]),
             "sconv": np.ascontiguousarray(sconv[:, sl]), "slru": np.ascontiguousarray(slru[:, sl])}
        m.update(W)
        in_maps.append(m)
    res = run_bass_kernel_spmd(nc, in_maps, core_ids=list(range(8))).results
    cat = lambda name, ax: np.concatenate([res[c][name] for c in range(8)], axis=ax)
    stack4 = lambda name: np.stack([res[c][name] for c in range(4)], axis=1)
    y_prompt = np.stack([res[c]["yp"] for c in range(4)], axis=0)
    y_sample = cat("ys", 0).reshape(32, LS, D)
    k_p = stack4("kp").reshape(2, 4, 128, 2, 64)
    v_p = stack4("vp").reshape(2, 4, 128, 2, 64)
    sre_p, sim_p = stack4("srep"), stack4("simp")
    conv_p, lru_p = stack4("convp"), stack4("lrup")
    k_s = cat("ks", 1).reshape(2, 32, LS, 2, 64)
    v_s = cat("vs", 1).reshape(2, 32, LS, 2, 64)
    sre_s, sim_s = cat("sres", 1), cat("sims", 1)
    conv_s, lru_s = cat("convs", 1), cat("lrus", 1)
    outs = (y_prompt, y_sample, k_p, v_p, sre_p, sim_p, conv_p, lru_p, k_s, v_s, sre_s, sim_s, conv_s, lru_s)
    return tuple(np.ascontiguousarray(o, dtype=np.float32) for o in outs)
```

```python
import math
import numpy as np
import concourse.bass as bass
import concourse.mybir as mybir
from concourse.bass_utils import run_bass_kernel_spmd

F32 = mybir.dt.float32
BF16 = mybir.dt.bfloat16
I32 = mybir.dt.int32
AF = mybir.ActivationFunctionType
ALU = mybir.AluOpType

D = 1024
DFF = 2816
NFF = 22
SEQ = 4096
TT = 512
NPT = SEQ // TT
NS = 4
LS = 16
TS = 128
EPS = 1e-6
OFF_Q, OFF_K, OFF_V, OFF_LX, OFF_LG = 256, 768, 896, 1024, 1280
TWO_PI = 2.0 * math.pi


class _Stop(Exception):
    pass


class Trk:
    def __init__(self, nc):
        self.nc = nc
        self.eng = {"pe": nc.tensor, "dve": nc.vector, "act": nc.scalar, "pool": nc.gpsimd, "sp": nc.sync}
        self.sem, self.cnt, self.waited, self.lastw, self.readers = {}, {}, {}, {}, {}
        self._stack = []
        for e in self.eng:
            self.newsem(e)

    def newsem(self, name):
        cm = self.nc.semaphore(name)
        self._stack.append(cm)
        self.sem[name] = cm.__enter__()
        self.cnt[name] = 0
        return name

    def _wait(self, e, toks):
        need = {}
        for (s, c) in toks:
            if c > need.get(s, 0):
                need[s] = c
        for s, c in need.items():
            if self.waited.get((e, s), 0) >= c:
                continue
            if s == e and e == "pe":
                continue
            self.eng[e].wait_ge(self.sem[s], c)
            self.waited[(e, s)] = c

    def deps(self, reads, writes):
        toks = []
        for r in reads:
            if r in self.lastw:
                toks.append(self.lastw[r])
        for w in writes:
            if w in self.lastw:
                toks.append(self.lastw[w])
            toks.extend(self.readers.get(w, []))
        return toks

    def commit(self, tok, reads, writes):
        for r in reads:
            self.readers.setdefault(r, []).append(tok)
        for w in writes:
            self.lastw[w] = tok
            self.readers[w] = []

    def op(self, e, fn, reads=(), writes=(), signal=True):
        self._wait(e, self.deps(reads, writes))
        inst = fn()
        if signal:
            self.cnt[e] += 1
            inst.then_inc(self.sem[e], 1)
            tok = (e, self.cnt[e])
        else:
            tok = (e, self.cnt[e] + 1)
        self.commit(tok, reads, writes)
        return tok

    def dma(self, q, out, in_, reads=(), writes=(), sem=None, **kw):
        self._wait(q, self.deps(reads, writes))
        if sem is None or sem in ("ld", "st", "ld2"):
            if not hasattr(self, "pool"):
                self.pool = [self.newsem(f"dp{i}") for i in range(32)]
                self.pool_i = 0
            sem = self.pool[self.pool_i % len(self.pool)]
            self.pool_i += 1
            if self.cnt[sem] > 0:
                self._wait(q, [(sem, self.cnt[sem])])
        inst = self.eng[q].dma_start(out=out, in_=in_, **kw)
        self.cnt[sem] += 16
        inst.then_inc(self.sem[sem], 16)
        tok = (sem, self.cnt[sem])
        self.commit(tok, reads, writes)
        return tok


WNAMES = ["ffn1_norm", "ffn1_w_gate", "ffn1_w_up", "ffn1_w_down", "mix_norm", "w_in",
          "ssm_a_re", "ssm_a_im", "ssm_log_dt", "ssm_b_re", "ssm_b_im", "ssm_c_re", "ssm_c_im", "ssm_d",
          "ssm_w_glu", "attn_sink", "conv_w", "conv_b", "lru_w_a", "lru_b_a", "lru_w_x", "lru_b_x",
          "lru_lambda", "out_norm", "w_out", "ffn2_norm", "ffn2_w_gate", "ffn2_w_up", "ffn2_w_down",
          "final_norm"]
WSHAPES = {
    "ffn1_norm": (2, D), "ffn1_w_gate": (2, D, DFF), "ffn1_w_up": (2, D, DFF), "ffn1_w_down": (2, DFF, D),
    "mix_norm": (2, D), "w_in": (2, D, 1536), "ssm_a_re": (2, 16, 64), "ssm_a_im": (2, 16, 64),
    "ssm_log_dt": (2, 16), "ssm_b_re": (2, 16, 64, 16), "ssm_b_im": (2, 16, 64, 16),
    "ssm_c_re": (2, 16, 16, 64), "ssm_c_im": (2, 16, 16, 64), "ssm_d": (2, 256), "ssm_w_glu": (2, 256, 256),
    "attn_sink": (2, 8), "conv_w": (2, 4, 256), "conv_b": (2, 256), "lru_w_a": (2, 4, 64, 64),
    "lru_b_a": (2, 256), "lru_w_x": (2, 4, 64, 64), "lru_b_x": (2, 256), "lru_lambda": (2, 256),
    "out_norm": (2, D), "w_out": (2, D, D), "ffn2_norm": (2, D), "ffn2_w_gate": (2, D, DFF),
    "ffn2_w_up": (2, D, DFF), "ffn2_w_down": (2, DFF, D), "final_norm": (D,),
}
OUT_SHAPES = {
    "yp": (SEQ, D), "ys": (NS * LS, D), "kp": (2, 128, 128), "vp": (2, 128, 128),
    "srep": (2, 16, 64), "simp": (2, 16, 64), "convp": (2, 3, 256), "lrup": (2, 256),
    "ks": (2, NS, LS, 128), "vs": (2, NS, LS, 128), "sres": (2, NS, 16, 64), "sims": (2, NS, 16, 64),
    "convs": (2, NS, 3, 256), "lrus": (2, NS, 256),
}


def build_program(n_ptiles=NPT, do_sample=True, stop=None):
    nc = bass.Bass("TRN2", target_bir_lowering=False)
    t = Trk(nc)

    def din(name, shape):
        return nc.dram_tensor(name, list(shape), F32, kind="ExternalInput").ap()

    def dout(name, shape):
        return nc.dram_tensor(name, list(shape), F32, kind="ExternalOutput").ap()

    I = {}
    I["xp"] = din("xp", (SEQ, D))
    I["xs"] = din("xs", (NS * LS, D))
    I["ck"] = din("ck", (2, NS, 128, 128))
    I["cv"] = din("cv", (2, NS, 128, 128))
    I["sre"] = din("sre", (2, NS, 16, 64))
    I["sim"] = din("sim", (2, NS, 16, 64))
    I["sconv"] = din("sconv", (2, NS, 3, 256))
    I["slru"] = din("slru", (2, NS, 256))
    for n in WNAMES:
        I[n] = din(n, WSHAPES[n])
    O = {n: dout(n, s) for n, s in OUT_SHAPES.items()}
    if stop is not None:
        O["dbg"] = dout("dbg", (128, 8192))
    dbg_off = [0]

    def dump(ap2d, key, width):
        dst = O["dbg"][:, dbg_off[0]:dbg_off[0] + width]
        if len(ap2d.shape) == 3:
            dst = dst.rearrange("p (c t) -> p c t", c=ap2d.shape[1])
        t.dma("sp", dst, ap2d, reads=key, sem=st)
        dbg_off[0] += width

    def finish():
        for sname in ring_sem + getattr(t, "pool", []):
            if t.cnt[sname] > 0:
                nc.sync.wait_ge(t.sem[sname], t.cnt[sname])
        return nc
    s5scr_t = nc.dram_tensor("s5scr_t", [2, 128, 4096], F32, kind="Internal").ap()
    s5scr_w = nc.dram_tensor("s5scr_w", [2, 128, 8192], BF16, kind="Internal").ap()

    def sb(name, shape, dt=F32):
        return nc.sbuf_tensor(name, list(shape), dt).__enter__()

    def dsem(name):
        return t.newsem(name)

    _once = {}

    def sb_once(name, shape, dt=F32):
        if name not in _once:
            _once[name] = sb(name, shape, dt)
        return _once[name]

    xT = sb("xT", [128, 8, TT])
    hT = sb("hT", [128, 8, TT], BF16)
    act = sb("act", [128, NFF, TT], BF16)
    ycat = sb("ycat", [128, 8, TT])
    R = 5
    RSZ = 2816
    ring = [sb(f"ring{i}", [128, RSZ], BF16) for i in range(R)]
    ring_sem = [dsem(f"rs{i}") for i in range(R)]
    s5t = sb("s5t", [128, 2, 16, TS])
    s5w = sb("s5w", [128, 4, 16, 128], BF16)
    xtm2 = sb("xtm2", [128, 2, D])
    xtm = [xtm2[:, 0, :], xtm2[:, 1, :]]
    sqb = [sb(f"sqb{i}", [128, TT], BF16) for i in range(2)]
    rstd = sb("rstd", [128, TT])
    rs1 = rstd
    sg = [sb(f"sg{i}", [128, TT]) for i in range(2)]
    uT = sb("uT", [128, 2, TT])
    ubf = sb("ubf", [128, 2, TT], BF16)
    qT = sb("qT", [128, 4, TT], BF16)
    kkT = sb("kkT", [128, 2, 128 + TT], BF16)
    vtm = sb("vtm", [128, 1 + TT // 128, 256], BF16)
    vnew = sb("vnew", [16, NS, 256], BF16)
    xlT = sb("xlT", [128, 2, 3 + TT])
    gg = sb("gg", [128, 2, TT])
    pT = [sb(f"pT{i}", [128, 2, 256], BF16) for i in range(2)]
    dn_sb = [sb(f"dnsb{i}", [128, 256]) for i in range(2)]
    a32 = sb("a32", [128, 16])
    b32 = sb("b32", [128, 16])
    zb = sb("zb", [128, 2, TT], BF16)
    sig = sb("sig", [128, TT])
    xc = sb("xc", [128, TT])
    xcb = sb("xcb", [128, TT], BF16)
    lrut = sb("lrut", [128, 4, TT])
    rr, ii, aa, mm = lrut[:, 0, :], lrut[:, 1, :], lrut[:, 2, :], lrut[:, 3, :]
    decb = xtm2[:, :, :].rearrange("p a t -> p (a t)")
    kvx = sb("kvx", [128, 512])
    ckf = sb("ckf", [128, 128])
    ckd = sb("ckd", [128, 2, 128])
    cvf = sb("cvf", [128, 128])
    ckT = sb("ckT", [128, 2, 128], BF16)
    hs_p = sb("hs_p", [128, 2, 16])
    hs_s = sb("hs_s", [128, 2, NS, 16])
    hcur = sb("hcur", [128, 16])
    lru_p = sb("lru_p", [128, 2, 2])
    lru_s = sb("lru_s", [128, 2, NS, 2])
    convh_p = sb("convh_p", [128, 2, 2, 3])
    kk_halo = sb("kk_halo", [128, 2, 2, 128], BF16)
    v_halo = sb("v_halo", [128, 2, 256], BF16)
    idn = sb("idn", [128, 128])
    idn2 = sb("idn2", [128, 128])
    iof = sb("iof", [128, 128])
    ones_m = sb("ones_m", [128, 3, 128], BF16)
    ones_b = sb("ones_b", [128, 128], BF16)
    Pm = sb("Pm", [128, 128])
    iot = sb("iot", [128, TS])
    gN = sb("gN", [128, 2, 4, 8])
    gF = sb("gF", [128, 8])
    dD = sb("dD", [128, 2, 2])
    cw = sb("cw", [128, 2, 4, 2])
    cb = sb("cb", [128, 2, 2])
    ba = sb("ba", [128, 2, 2])
    bx = sb("bx", [128, 2, 2])
    lam = sb("lam", [128, 2, 2])
    c1 = sb("c1", [128, 2, 2])
    c2 = sb("c2", [128, 2, 2])
    esk = sb("esk", [128, 2, 8])
    esink = sb("esink", [128, 2, 2, 256])
    wglu = sb("wglu", [128, 2, 2, 256], BF16)
    wlru = sb("wlru", [128, 2, 4, 128], BF16)
    rdec = sb("rdec", [128, 2, 16])
    pa_re = sb("pa_re", [128, 16])
    pa_im = sb("pa_im", [128, 16])
    pdt = sb("pdt", [128, 16])
    tmpA = sb("tmpA", [128, 16])
    tmpB = sb("tmpB", [128, 16])
    tmpC = sb("tmpC", [128, 16])
    tmpD = sb("tmpD", [128, 16])
    tmpi = sb("tmpi", [128, 16], I32)
    lbre = sb("lbre", [128, 16])
    lbim = sb("lbim", [128, 16])
    kre = sb("kre", [128, 16])
    kim = sb("kim", [128, 16])
    K1 = sb("K1", [128, 16])
    K2 = sb("K2", [128, 16])
    K3 = sb("K3", [128, 16])
    K4 = sb("K4", [128, 16])
    actf = act[:, :, :].rearrange("p j t -> p (j t)").bitcast(F32)
    zbuf = actf[:, 0:2048]
    wbuf = actf[:, 2048:4096]
    tbuf = actf[:, 4096:5120]
    hTf = hT[:, :, :].rearrange("p c t -> p (c t)")
    Abuf = hTf[:, 0:2048]
    Bbuf = hTf[:, 2048:4096]
    bre2 = actf[:, 0:256].rearrange("p (g h) -> p g h", g=16)
    bim2 = actf[:, 256:512].rearrange("p (g h) -> p g h", g=16)
    BB = actf[:, 512:1024].rearrange("p (a f) -> p a f", a=2)
    BBt = actf[:, 1024:1280]
    cc_in = actf[:, 1280:1536].rearrange("p (c r q) -> p c r q", c=2, r=2)
    CC = actf[:, 1536:1792].rearrange("p (a f) -> p a f", a=2)

    pX = nc.psum_tensor("pX", [128, 2048], F32).__enter__()
    PS = {n: nc.psum_tensor(n, [128, 512], F32).__enter__() for n in ["pO0", "pO1", "pM0", "pM1"]}
    for i_, n in enumerate(["pA0", "pA1", "pB0", "pB1"]):
        PS[n] = pX[:, i_ * 512:(i_ + 1) * 512]
    rot = {"pA": 0, "pB": 0, "pO": 0, "pM": 0}

    def bank(kind):
        i = rot[kind]
        rot[kind] ^= 1
        n = f"{kind}{i}"
        return PS[n], n

    V, A_, P_, PE_ = nc.vector, nc.scalar, nc.gpsimd, nc.tensor
    ld = dsem("ld")
    st = dsem("st")

    def mm_(out, lhsT, rhs, start, stop, reads, writes, signal):
        t.op("pe", lambda: PE_.matmul(out, lhsT=lhsT, rhs=rhs, start=start, stop=stop), reads=reads, writes=writes,
             signal=signal)

    t.op("pool", lambda: P_.iota(iof[:], pattern=[[1, 128]], base=0, channel_multiplier=-1,
                                 allow_small_or_imprecise_dtypes=True), writes=["iof"])
    t.op("dve", lambda: V.tensor_single_scalar(idn[:], iof[:], 0.0, ALU.is_equal), reads=["iof"], writes=["idn"])
    t.op("dve", lambda: V.tensor_single_scalar(Pm[:], iof[:], 64.0, ALU.is_equal), reads=["iof"], writes=["Pm"])
    t.op("dve", lambda: V.tensor_single_scalar(idn2[:], iof[:], -64.0, ALU.is_equal), reads=["iof"], writes=["idn2"])
    t.op("dve", lambda: V.tensor_tensor(Pm[:], Pm[:], idn2[:], ALU.subtract), reads=["Pm", "idn2"], writes=["Pm"])
    t.op("pool", lambda: P_.iota(iot[:], pattern=[[1, TS]], base=1, channel_multiplier=0,
                                 allow_small_or_imprecise_dtypes=True), writes=["iot"])
    for i, v in enumerate([1.0 / 1024, 1.0 / 512, 1.0 / 256]):
        t.op("dve", lambda i=i, v=v: V.memset(ones_m[:, i, :], v), writes=[("ones_m", i)])
    t.op("dve", lambda: V.memset(ones_b[:], 1.0), writes=["ones_b"])
    for nm in ["hs_p", "lru_p", "convh_p"]:
        pass
    t.op("dve", lambda: V.memset(hs_p[:], 0.0), writes=[("hs_p", 0), ("hs_p", 1)])
    t.op("dve", lambda: V.memset(lru_p[:], 0.0), writes=[("lru_p", 0), ("lru_p", 1)])
    t.op("dve", lambda: V.memset(convh_p[:], 0.0), writes=[("convh_p", 0), ("convh_p", 1)])
    t.op("dve", lambda: V.memset(wlru[:], 0.0), writes=["wlru"])
    ld2 = dsem("ld2")

    def ldc(dst, src, wkey, **kw):
        t.dma("sp", dst, src, writes=[wkey], sem=ld, **kw)

    NC_ = dict(allow_slow_non_contiguous=True)
    for l in range(2):
        for k, nm in enumerate(["ffn1_norm", "mix_norm", "out_norm", "ffn2_norm"]):
            ldc(gN[:, l, k, :], I[nm][l].rearrange("(c p) -> p c", p=128), "gN", **NC_)
        ldc(dD[:, l, :], I["ssm_d"][l].rearrange("(c p) -> p c", p=128), "dD", **NC_)
        for tt_ in range(4):
            ldc(cw[:, l, tt_, :], I["conv_w"][l, tt_].rearrange("(c p) -> p c", p=128), "cw", **NC_)
        ldc(cb[:, l, :], I["conv_b"][l].rearrange("(c p) -> p c", p=128), "cb", **NC_)
        ldc(ba[:, l, :], I["lru_b_a"][l].rearrange("(c p) -> p c", p=128), "ba", **NC_)
        ldc(bx[:, l, :], I["lru_b_x"][l].rearrange("(c p) -> p c", p=128), "bx", **NC_)
        ldc(lam[:, l, :], I["lru_lambda"][l].rearrange("(c p) -> p c", p=128), "lam", **NC_)
        ldc(esk[:, l, :], I["attn_sink"][l:l + 1, :].to_broadcast([128, 8]), "esk", **NC_)
        t.dma("pool", wglu[:, l, :, :], I["ssm_w_glu"][l].rearrange("(c p) f -> p c f", p=128), writes=["wglu"], sem=ld2)
        for wi_, nm in enumerate(["lru_w_a", "lru_w_x"]):
            for n_ in range(4):
                c_, hlf = n_ // 2, n_ % 2
                t.dma("pool", wlru[hlf * 64:(hlf + 1) * 64, l, wi_ * 2 + c_, hlf * 64:(hlf + 1) * 64], I[nm][l, n_], writes=["wlru"], sem=ld2)
    ldc(gF[:], I["final_norm"].rearrange("(c p) -> p c", p=128), "gF", **NC_)
    t.op("act", lambda: A_.activation(c1[:], lam[:], AF.Exp, scale=-1.0), reads=["lam"], writes=["c1"])
    t.op("act", lambda: A_.activation(c1[:], c1[:], AF.Ln, bias=1.0, scale=1.0), reads=["c1"], writes=["c1"])
    t.op("dve", lambda: V.tensor_scalar(c2[:], c1[:], -16.0, None, ALU.mult), reads=["c1"], writes=["c2"])
    t.op("dve", lambda: V.tensor_scalar(c1[:], c1[:], -8.0, None, ALU.mult), reads=["c1", "c2"], writes=["c1"])
    t.op("act", lambda: A_.activation(esk[:], esk[:], AF.Exp), reads=["esk"], writes=["esk"])
    for l in range(2):
        for h in range(2):
            for gp in range(2):
                for i in range(2):
                    g = 2 * i + gp
                    t.op("dve", lambda l=l, h=h, gp=gp, i=i, g=g: V.tensor_copy(
                        esink[:, l, h, gp * 128 + i * 64: gp * 128 + i * 64 + 64],
                        esk[:, l, 4 * h + g: 4 * h + g + 1].to_broadcast([128, 64])), reads=["esk"], writes=["esink"])

    esbP = sb("esbP", [1, 2, 2, 256], BF16)
    esbS = sb("esbS", [1, 2, 2, 64], BF16)
    t.op("dve", lambda: V.tensor_copy(esbP[:], esink[0:1, :, :, :]), reads=["esink"], writes=["esb"])
    for gp in range(2):
        for i in range(2):
            t.op("dve", lambda gp=gp, i=i: V.tensor_copy(esbS[:, :, :, gp * 32 + i * 16: gp * 32 + i * 16 + 16],
                                                         esink[0:1, :, :, gp * 128 + i * 64: gp * 128 + i * 64 + 16]), reads=["esink"], writes=["esb"])

    def build_s5(l):
        for hf in range(2):
            ps_ = slice(hf * 64, hf * 64 + 64)
            ldc(pa_re[ps_, :], I["ssm_a_re"][l].rearrange("g p -> p g"), "pa_re", **NC_)
            ldc(pa_im[ps_, :], I["ssm_a_im"][l].rearrange("g p -> p g"), "pa_im", **NC_)
            ldc(bre2[ps_, :, :], I["ssm_b_re"][l].rearrange("g p h -> p g h"), "bre2", **NC_)
            ldc(bim2[ps_, :, :], I["ssm_b_im"][l].rearrange("g p h -> p g h"), "bim2", **NC_)
        ldc(pdt[:], I["ssm_log_dt"][l:l + 1, :].to_broadcast([128, 16]), "pdt", **NC_)
        for ch in range(2):
            ldc(cc_in[:, ch, 0, :], I["ssm_c_re"][l, 8 * ch:8 * ch + 8].rearrange("g h p -> (g h) p"), "cc_in")
            ldc(cc_in[:, ch, 1, :], I["ssm_c_im"][l, 8 * ch:8 * ch + 8].rearrange("g h p -> (g h) p"), "cc_in")
        dv = lambda fn, r, w: t.op("dve", fn, reads=r, writes=w)
        ac = lambda fn, r, w: t.op("act", fn, reads=r, writes=w)
        ac(lambda: A_.activation(pdt[:], pdt[:], AF.Exp), ["pdt"], ["pdt"])
        dv(lambda: V.tensor_tensor(tmpA[:], pa_re[:], pdt[:], ALU.mult), ["pa_re", "pdt"], ["tmpA"])
        ac(lambda: A_.activation(rdec[:, l, :], tmpA[:], AF.Exp), ["tmpA"], [("rdec", l)])
        dv(lambda: V.tensor_tensor(tmpB[:], pa_im[:], pdt[:], ALU.mult), ["pa_im", "pdt"], ["tmpB"])
        dv(lambda: V.tensor_scalar(tmpB[:], tmpB[:], 1.0 / TWO_PI, None, ALU.mult), ["tmpB"], ["tmpB"])

        def sincos(dst, dkey, shift):
            dv(lambda: V.tensor_scalar(tmpC[:], tmpB[:], shift, None, ALU.add), ["tmpB"], ["tmpC"])
            dv(lambda: V.tensor_copy(tmpi[:], tmpC[:]), ["tmpC"], ["tmpi"])
            dv(lambda: V.tensor_copy(tmpD[:], tmpi[:]), ["tmpi"], ["tmpD"])
            dv(lambda: V.tensor_tensor(tmpC[:], tmpC[:], tmpD[:], ALU.subtract), ["tmpC", "tmpD"], ["tmpC"])
            ac(lambda: A_.activation(dst, tmpC[:], AF.Sin, scale=TWO_PI), ["tmpC"], [dkey])

        sincos(lbim[:], "lbim_k", 0.0)
        sincos(lbre[:], "lbre_k", 0.25)
        dv(lambda: V.tensor_tensor(lbre[:], lbre[:], rdec[:, l, :], ALU.mult), ["lbre_k", ("rdec", l)], ["lbre_k"])
        dv(lambda: V.tensor_tensor(lbim[:], lbim[:], rdec[:, l, :], ALU.mult), ["lbim_k", ("rdec", l)], ["lbim_k"])
        dv(lambda: V.tensor_tensor(tmpA[:], pa_re[:], pa_re[:], ALU.mult), ["pa_re"], ["tmpA"])
        dv(lambda: V.tensor_tensor(tmpC[:], pa_im[:], pa_im[:], ALU.mult), ["pa_im"], ["tmpC"])
        dv(lambda: V.tensor_tensor(tmpA[:], tmpA[:], tmpC[:], ALU.add), ["tmpA", "tmpC"], ["tmpA"])
        dv(lambda: V.reciprocal(tmpA[:], tmpA[:]), ["tmpA"], ["tmpA"])
        dv(lambda: V.tensor_scalar(tmpC[:], lbre[:], -1.0, None, ALU.add), ["lbre_k"], ["tmpC"])
        dv(lambda: V.tensor_tensor(kre[:], tmpC[:], pa_re[:], ALU.mult), ["tmpC", "pa_re"], ["kre"])
        dv(lambda: V.tensor_tensor(tmpD[:], lbim[:], pa_im[:], ALU.mult), ["lbim_k", "pa_im"], ["tmpD"])
        dv(lambda: V.tensor_tensor(kre[:], kre[:], tmpD[:], ALU.add), ["kre", "tmpD"], ["kre"])
        dv(lambda: V.tensor_tensor(kre[:], kre[:], tmpA[:], ALU.mult), ["kre", "tmpA"], ["kre"])
        dv(lambda: V.tensor_tensor(kim[:], lbim[:], pa_re[:], ALU.mult), ["lbim_k", "pa_re"], ["kim"])
        dv(lambda: V.tensor_tensor(tmpD[:], tmpC[:], pa_im[:], ALU.mult), ["tmpC", "pa_im"], ["tmpD"])
        dv(lambda: V.tensor_tensor(kim[:], kim[:], tmpD[:], ALU.subtract), ["kim", "tmpD"], ["kim"])
        dv(lambda: V.tensor_tensor(kim[:], kim[:], tmpA[:], ALU.mult), ["kim", "tmpA"], ["kim"])
        lo, hi = slice(0, 64), slice(64, 128)
        dv(lambda: V.tensor_copy(K1[lo, :], kre[lo, :]), ["kre"], ["K1"])
        dv(lambda: V.tensor_copy(K1[hi, :], kim[hi, :]), ["kim"], ["K1"])
        dv(lambda: V.tensor_scalar(K2[lo, :], kim[lo, :], -1.0, None, ALU.mult), ["kim"], ["K2"])
        dv(lambda: V.tensor_copy(K2[hi, :], kre[hi, :]), ["kre"], ["K2"])
        dv(lambda: V.tensor_copy(K3[lo, :], kim[lo, :]), ["kim"], ["K3"])
        dv(lambda: V.tensor_scalar(K3[hi, :], kre[hi, :], -1.0, None, ALU.mult), ["kre"], ["K3"])
        dv(lambda: V.tensor_copy(K4[lo, :], kre[lo, :]), ["kre"], ["K4"])
        dv(lambda: V.tensor_copy(K4[hi, :], kim[hi, :]), ["kim"], ["K4"])
        bcast = lambda k: k[:, :].unsqueeze(2).to_broadcast([128, 16, 16])
        for which, (Ka, Kb) in enumerate([(K1, K2), (K3, K4)]):
            BBv = BB[:, which, :].rearrange("p (g h) -> p g h", g=16)
            BBtv = BBt[:, :].rearrange("p (g h) -> p g h", g=16)
            dv(lambda BBv=BBv, Ka=Ka: V.tensor_tensor(BBv, bre2[:], bcast(Ka), ALU.mult), ["bre2", "K1", "K2", "K3", "K4"], [("BB", which)])
            dv(lambda BBtv=BBtv, Kb=Kb: V.tensor_tensor(BBtv, bim2[:], bcast(Kb), ALU.mult), ["bim2", "K1", "K2", "K3", "K4"], ["BBt"])
            dv(lambda which=which: V.tensor_tensor(BB[:, which, :], BB[:, which, :], BBt[:], ALU.add), [("BB", which), "BBt"], [("BB", which)])
        dv(lambda: V.memset(s5w[:], 0.0), [], ["s5w"])
        for which in range(2):
            for ch in range(2):
                pt_, pk = bank("pM")
                t.op("pe", lambda pt_=pt_, which=which, ch=ch: PE_.transpose(pt_[:, 0:128], BB[:, which, ch * 128:(ch + 1) * 128], idn[:]),
                     reads=[("BB", which), "idn"], writes=[pk])
                for g8 in range(8):
                    g = ch * 8 + g8
                    rows = slice(16 * g8, 16 * g8 + 16)
                for g8 in range(8):
                    g = ch * 8 + g8
                    dv(lambda pt_=pt_, which=which, g=g, g8=g8: V.tensor_scalar(
                        s5w[:, which, g, :], pt_[:, 0:128], gmask[:, g8:g8 + 1], None, ALU.mult),
                        [pk, "gmask"], ["s5w"])
        for ch in range(2):
            dv(lambda ch=ch: V.tensor_copy(CC[:, 0, 0:64], cc_in[:, ch, 0, :]), ["cc_in"], ["CC"])
            dv(lambda ch=ch: V.tensor_scalar(CC[:, 0, 64:128], cc_in[:, ch, 1, :], -1.0, None, ALU.mult), ["cc_in"], ["CC"])
            dv(lambda ch=ch: V.tensor_scalar(CC[:, 1, 0:64], cc_in[:, ch, 1, :], -1.0, None, ALU.mult), ["cc_in"], ["CC"])
            dv(lambda ch=ch: V.tensor_scalar(CC[:, 1, 64:128], cc_in[:, ch, 0, :], -1.0, None, ALU.mult), ["cc_in"], ["CC"])
            for which in range(2):
                pt_, pk = bank("pM")
                t.op("pe", lambda pt_=pt_, which=which: PE_.transpose(pt_[:, 0:128], CC[:, which, :], idn[:]),
                     reads=["CC", "idn"], writes=[pk])
                for g8 in range(8):
                    g = ch * 8 + g8
                    dv(lambda pt_=pt_, which=which, g=g, g8=g8: V.tensor_copy(
                        s5w[:, 2 + which, g, 16 * g8:16 * g8 + 16], pt_[:, 16 * g8:16 * g8 + 16]), [pk], ["s5w"])
        bigu = ycat[:, 0:4, :].rearrange("p c t -> p (c t)")
        bigk = xT[:, 0:4, :].rearrange("p c t -> p (c t)")
        bigi = ycat[:, 4:8, :].rearrange("p c t -> p (c t)").bitcast(I32)
        KU = [("ycat", c) for c in range(4)]
        KK = [("xT", c) for c in range(4)]
        KI = [("ycat", c) for c in range(4, 8)]
        bu3 = bigu.rearrange("p (g j) -> p g j", g=16)
        dv(lambda: V.tensor_tensor(bu3, iot[:, :].unsqueeze(1).to_broadcast([128, 16, TS]),
                                   tmpB[:, :].unsqueeze(2).to_broadcast([128, 16, TS]), ALU.mult), ["iot", "tmpB"], KU)
        for which, shift in [(1, 0.0), (0, 0.25)]:
            dv(lambda shift=shift: V.tensor_scalar(bigk, bigu, shift, None, ALU.add), KU, KK)
            dv(lambda: V.tensor_copy(bigi, bigk), KK, KI)
            dv(lambda: V.tensor_copy(s5t[:, which, :, :].rearrange("p g j -> p (g j)"), bigi), KI, ["s5t"])
            dv(lambda which=which: V.tensor_tensor(bigk, bigk, s5t[:, which, :, :].rearrange("p g j -> p (g j)"), ALU.subtract),
               KK + ["s5t"], KK)
            ac(lambda which=which: A_.activation(s5t[:, which, :, :].rearrange("p g j -> p (g j)"), bigk, AF.Sin, scale=TWO_PI),
               KK, ["s5t"])
        t.dma("sp", s5scr_t[l], s5t[:, :, :, :].rearrange("p a g j -> p (a g j)"), reads=["s5t"], writes=[("scrt", l)], sem=st)
        t.dma("sp", s5scr_w[l], s5w[:, :, :, :].rearrange("p a g j -> p (a g j)"), reads=["s5w"], writes=[("scrw", l)], sem=st)

    gmask = sb("gmask", [128, 8])
    gm_t = sb("gm_t", [128, 8])
    t.op("pool", lambda: P_.iota(gm_t[:], pattern=[[16, 8]], base=0, channel_multiplier=-1,
                                 allow_small_or_imprecise_dtypes=True), writes=["gm_t"])
    t.op("dve", lambda: V.tensor_single_scalar(gmask[:], gm_t[:], 0.5, ALU.is_le), reads=["gm_t"], writes=["gmask"])
    t.op("dve", lambda: V.tensor_single_scalar(gm_t[:], gm_t[:], -15.5, ALU.is_ge), reads=["gm_t", "gmask"], writes=["gm_t"])
    t.op("dve", lambda: V.tensor_tensor(gmask[:], gmask[:], gm_t[:], ALU.mult), reads=["gm_t", "gmask"], writes=["gmask"])

    for l in range(2):
        build_s5(l)
        if stop == "const" and l == 0:
            dump(s5t[:, :, :, :].rearrange("p a g j -> p (a g j)"), ["s5t"], 4096)
            dump(rdec[:, 0, :], [("rdec", 0)], 16)
            dump(kre[:], ["kre"], 16)
            dump(kim[:], ["kim"], 16)
            dump(esink[:, 0, :, :].rearrange("p h c -> p (h c)"), ["esink"], 512)
            dump(c1[:, :, :].rearrange("p l c -> p (l c)"), ["c1"], 4)
            dump(gmask[:], ["gmask"], 8)
            dump(gN[:, :, :, :].rearrange("p l k c -> p (l k c)"), ["gN"], 64)
    if stop == "const":
        return finish()
    ftoks = []
    for k in ["bre2", "bim2", ("BB", 0), ("BB", 1), "BBt", "cc_in", "CC"]:
        if k in t.lastw:
            ftoks.append(t.lastw[k])
        ftoks.extend(t.readers.get(k, []))
    for j in range(NFF):
        t.readers.setdefault(("act", j), []).extend(ftoks)

    def load_s5(l):
        t.dma("sp", s5t[:, :, :, :].rearrange("p a g j -> p (a g j)"), s5scr_t[l], reads=[("scrt", l)], writes=["s5t"], sem=ld)
        t.dma("sp", s5w[:, :, :, :].rearrange("p a g j -> p (a g j)"), s5scr_w[l], reads=[("scrw", l)], writes=["s5w"], sem=ld)

    S5R = ["s5t"]
    S5W = ["s5w"]

    def win_cols(c):
        if c < 2:
            return [(0, 128 * c, 128)]
        if c < 6:
            return [(0, OFF_Q + 128 * (c - 2), 128)]
        if c < 8:
            h = c - 6
            return [(0, OFF_K + 64 * h, 64), (64, OFF_K + 64 * h, 64)]
        if c < 10:
            return [(0, OFF_LX + 128 * (c - 8), 128)]
        return [(0, OFF_LG + 128 * (c - 10), 128)]

    items = []
    NIDENT = 2 * (NFF + 8 + 12 + 3 + 8 + NFF + 8)
    wscr = nc.dram_tensor("wscr", [NIDENT, 128, RSZ], BF16, kind="Internal").ap()
    seen = {}
    USED = {"gu": 2048, "dn": NFF * 128, "win": 1024, "wvv": 2048, "wkv": 2048, "wxl": 2048, "wout": 1024}

    def emit_item(it, slot, key, sem):
        kind = it[0]
        used = USED[kind]
        if it in seen:
            idx = seen[it]
            t.dma("pool", slot[:, 0:used], wscr[idx, :, 0:used], reads=[("wscr", idx)], writes=[key], sem=sem)
            return
        emit_item_f32(it, slot, key, sem)
        idx = len(seen)
        seen[it] = idx
        t.dma("sp", wscr[idx, :, 0:used], slot[:, 0:used], reads=[key], writes=[("wscr", idx)], sem=st)

    def emit_item_f32(it, slot, key, sem):
        kind = it[0]

        def wd(dst, src):
            t.dma("pool", dst, src, writes=[key], sem=sem)
        if kind == "gu":
            _, l, f, j = it
            wg, wu = I[f"ffn{f}_w_gate"][l], I[f"ffn{f}_w_up"][l]
            wd(slot[:, 0:1024].rearrange("p (k f) -> p k f", k=8), wg[:, j * 128:(j + 1) * 128].rearrange("(k p) f -> p k f", p=128))
            wd(slot[:, 1024:2048].rearrange("p (k f) -> p k f", k=8), wu[:, j * 128:(j + 1) * 128].rearrange("(k p) f -> p k f", p=128))
        elif kind == "dn":
            _, l, f, m = it
            w_ = I[f"ffn{f}_w_down"][l]
            wd(slot[:, 0:NFF * 128].rearrange("p (j f) -> p j f", j=NFF), w_[:, m * 128:(m + 1) * 128].rearrange("(j p) f -> p j f", p=128))
        elif kind == "win":
            _, l, c = it
            for (dc, sc, wdt) in win_cols(c):
                wd(slot[:, 0:1024].rearrange("p (k f) -> p k f", k=8)[:, :, dc:dc + wdt],
                   I["w_in"][l][:, sc:sc + wdt].rearrange("(k p) f -> p k f", p=128))
        elif kind == "wvv":
            _, l = it
            for q_, h in enumerate([0, 0, 1, 1]):
                wd(slot[:, 0:2048].rearrange("p (k f) -> p k f", k=8)[:, :, 64 * q_:64 * q_ + 64],
                   I["w_in"][l][:, OFF_V + 64 * h:OFF_V + 64 * h + 64].rearrange("(k p) f -> p k f", p=128))
        elif kind == "wkv":
            _, l = it
            wd(slot[:, 0:2048].rearrange("p (k f) -> p k f", k=8), I["w_in"][l][:, OFF_K:OFF_K + 256].rearrange("(k p) f -> p k f", p=128))
        elif kind == "wxl":
            _, l = it
            wd(slot[:, 0:2048].rearrange("p (k f) -> p k f", k=8), I["w_in"][l][:, OFF_LX:OFF_LX + 256].rearrange("(k p) f -> p k f", p=128))
        elif kind == "wout":
            _, l, m = it
            wd(slot[:, 0:1024].rearrange("p (k f) -> p k f", k=8), I["w_out"][l][:, m * 128:(m + 1) * 128].rearrange("(k p) f -> p k f", p=128))

    tiles = [("P", i) for i in range(n_ptiles)]
    if do_sample:
        tiles.insert(min(1, len(tiles)), ("S", 0))

    def tile_needs_kvx(tile):
        return tile[0] == "S" or tile[1] == NPT - 1

    for tile in tiles:
        for l in range(2):
            for j in range(NFF):
                items.append(("gu", l, 1, j))
            for m in range(8):
                items.append(("dn", l, 1, m))
            for c in range(12):
                items.append(("win", l, c))
            items.append(("wvv", l))
            if tile_needs_kvx(tile):
                items.append(("wkv", l))
                items.append(("wxl", l))
            for m in range(8):
                items.append(("wout", l, m))
            for j in range(NFF):
                items.append(("gu", l, 2, j))
            for m in range(8):
                items.append(("dn", l, 2, m))
    wstate = {"next_emit": 0, "next_use": 0}

    def wnext(kind):
        idx = wstate["next_use"]
        assert items[idx][0] == kind, (items[idx], kind)
        lim = min(len(items), idx + R - 1)
        while wstate["next_emit"] < lim:
            i = wstate["next_emit"]
            k = i % R
            emit_item(items[i], ring[k], ("ring", k), ring_sem[k])
            wstate["next_emit"] += 1
        wstate["next_use"] += 1
        return ring[idx % R], ("ring", idx % R)

    XK = [("xT", c) for c in range(8)]
    HK = [("hT", c) for c in range(8)]
    ACTK = [("act", j) for j in range(NFF)]

    def fence(src_keys, dst_keys):
        toks = []
        for k in src_keys:
            if k in t.lastw:
                toks.append(t.lastw[k])
            toks.extend(t.readers.get(k, []))
        for k in dst_keys:
            t.readers.setdefault(k, []).extend(toks)

    def stat_acc(m, NT):
        sq = sqb[m % 2]
        sk = ("sqb", m % 2)
        t.op("act", lambda: A_.activation(sq[:, 0:NT], xT[:, m, 0:NT], AF.Square), reads=[("xT", m)], writes=[sk])
        todo = [m - 1] if m >= 1 else []
        if m == 7:
            todo.append(7)
        for mm2 in todo:
            mm_(PS["pA0"][:, 0:NT], ones_m[:, 0, :], sqb[mm2 % 2][:, 0:NT], mm2 == 0, mm2 == 7, [("sqb", mm2 % 2), ("ones_m", 0)], ["pA0"], True)

    def rmsnorm(src, srck, gains, NT, ngrp=None, pre=False):
        groups = ngrp or [(list(range(8)), 0)]
        for (chs, oi) in groups:
            if pre:
                pm, pk = PS["pA0"], "pA0"
            else:
                pm, pk = bank("pM")
            for n_, c in enumerate(chs):
                if pre:
                    break
                sq = sqb[n_ % 2]
                sk = ("sqb", n_ % 2)
                t.op("act", lambda sq=sq, c=c: A_.activation(sq[:, 0:NT], src[:, c, 0:NT], AF.Square), reads=[srck(c)], writes=[sk])
                mm_(pm[:, 0:NT], ones_m[:, oi, :], sq[:, 0:NT], n_ == 0, n_ == len(chs) - 1, [sk, ("ones_m", oi)], [pk], True)
            t.op("act", lambda pm=pm: A_.activation(rs1[:, 0:NT], pm[:, 0:NT], AF.Ln, bias=epsb[:, 0:1], scale=1.0), reads=[pk, "epsb"], writes=["rstd"])
            t.op("act", lambda: A_.activation(rstd[:, 0:NT], rs1[:, 0:NT], AF.Exp, scale=-0.5), reads=["rstd"], writes=["rstd"])
            for c in chs:
                t.op("dve", lambda c=c: V.scalar_tensor_tensor(hT[:, c, 0:NT], src[:, c, 0:NT], gains[:, c:c + 1], rstd[:, 0:NT], ALU.mult, ALU.mult),
                     reads=[srck(c), "rstd", "gN", "gF"], writes=[("hT", c)])

    epsb = sb("epsb", [128, 1])
    t.op("dve", lambda: V.memset(epsb[:], EPS), writes=["epsb"])

    def ffn(l, f, NT, pre=False):
        fence(["s5z", "s5w_", "s5tmp"], ACTK)
        fence(["s5A", "s5B"], HK)
        rmsnorm(xT, lambda c: ("xT", c), gN[:, l, 0 if f == 1 else 3, :], NT, pre=pre)
        for j in range(NFF):
            slot, sk = wnext("gu")
            pa, pak = bank("pA")
            pb, pbk = bank("pB")
            wg = slot[:, 0:1024].rearrange("p (k f) -> p k f", k=8)
            wu = slot[:, 1024:2048].rearrange("p (k f) -> p k f", k=8)
            for k in range(8):
                mm_(pa[:, 0:NT], wg[:, k, :], hT[:, k, 0:NT], k == 0, k == 7, [sk, ("hT", k)], [pak], k == 7)
            for k in range(8):
                mm_(pb[:, 0:NT], wu[:, k, :], hT[:, k, 0:NT], k == 0, k == 7, [sk, ("hT", k)], [pbk], k == 7)
            s_ = sg[j % 2]
            t.op("act", lambda pa=pa, s_=s_: A_.activation(s_[:, 0:NT], pa[:, 0:NT], AF.Silu), reads=[pak], writes=[("sg", j % 2)])
            t.op("dve", lambda pb=pb, s_=s_, j=j: V.tensor_tensor(act[:, j, 0:NT], s_[:, 0:NT], pb[:, 0:NT], ALU.mult),
                 reads=[("sg", j % 2), pbk], writes=[("act", j)])
        for m in range(8):
            slot, sk = wnext("dn")
            wd_ = slot[:, 0:NFF * 128].rearrange("p (j f) -> p j f", j=NFF)
            po, pok = bank("pO")
            for j in range(NFF):
                mm_(po[:, 0:NT], wd_[:, j, :], act[:, j, 0:NT], j == 0, j == NFF - 1, [sk, ("act", j)], [pok], j == NFF - 1)
            t.op("dve", lambda po=po, m=m: V.scalar_tensor_tensor(xT[:, m, 0:NT], po[:, 0:NT], 0.5, xT[:, m, 0:NT], ALU.mult, ALU.add),
                 reads=[pok, ("xT", m)], writes=[("xT", m)])
            stat_acc(m, NT)

    def load_x(src_rows, NT):
        fence(["s5dec"], [("xtm", 0), ("xtm", 1)])
        nb = (NT + 127) // 128
        for b in range(nb):
            rows = min(128, NT - b * 128)
            xb = xtm[b % 2]
            t.dma("sp", xb[0:rows, :], src_rows[b * 128:b * 128 + rows, :], writes=[("xtm", b % 2)], sem=ld)
            for half in range(2):
                pm, pk = bank("pM")
                for c4 in range(4):
                    c = half * 4 + c4
                    t.op("pe", lambda pm=pm, c4=c4, c=c, xb=xb, rows=rows: PE_.transpose(
                        pm[:, c4 * 128:c4 * 128 + rows], xb[0:rows, c * 128:(c + 1) * 128], idn[0:rows, 0:rows]),
                        reads=[("xtm", b % 2), "idn"], writes=[pk], signal=(c4 == 3))
                eng = "act" if half == 0 else "dve"
                outap = xT[:, half * 4:half * 4 + 4, b * 128:b * 128 + rows]
                inap = pm[:, :].rearrange("p (c t) -> p c t", c=4)[:, :, 0:rows]
                if eng == "act":
                    t.op("act", lambda outap=outap, inap=inap: A_.copy(outap, inap), reads=[pk], writes=[("xT", half * 4 + i) for i in range(4)])
                else:
                    t.op("dve", lambda outap=outap, inap=inap: V.tensor_copy(outap, inap), reads=[pk], writes=[("xT", half * 4 + i) for i in range(4)])

    def store_y(dst_rows, NT):
        YK = [("ycat", c) for c in range(8)]
        pm, pk = PS["pA0"], "pA0"
        t.op("act", lambda pm=pm: A_.activation(rs1[:, 0:NT], pm[:, 0:NT], AF.Ln, bias=epsb[:, 0:1], scale=1.0), reads=[pk, "epsb"], writes=["rstd"])
        t.op("act", lambda: A_.activation(rstd[:, 0:NT], rs1[:, 0:NT], AF.Exp, scale=-0.5), reads=["rstd"], writes=["rstd"])
        for c in range(8):
            t.op("dve", lambda c=c: V.scalar_tensor_tensor(ycat[:, c, 0:NT], xT[:, c, 0:NT], gF[:, c:c + 1], rstd[:, 0:NT], ALU.mult, ALU.mult),
                 reads=[("xT", c), "rstd", "gF"], writes=[("ycat", c)])
        nb = (NT + 127) // 128
        fence(["s5dec"], [("xtm", 0), ("xtm", 1)])
        for b in range(nb):
            rows = min(128, NT - b * 128)
            xb = xtm[b % 2]
            for half in range(2):
                pm, pk = bank("pM")
                for c4 in range(4):
                    c = half * 4 + c4
                    t.op("pe", lambda pm=pm, c4=c4, c=c, rows=rows, b=b: PE_.transpose(
                        pm[0:rows, c4 * 128:(c4 + 1) * 128], ycat[:, c, b * 128:b * 128 + rows], idn[:]),
                        reads=[("ycat", c), "idn"], writes=[pk], signal=(c4 == 3))
                if half == 0:
                    t.op("act", lambda pm=pm, xb=xb, rows=rows: A_.copy(xb[0:rows, 0:512], pm[0:rows, :]), reads=[pk], writes=[("xtm", b % 2)])
                else:
                    t.op("dve", lambda pm=pm, xb=xb, rows=rows: V.tensor_copy(xb[0:rows, 512:1024], pm[0:rows, :]), reads=[pk], writes=[("xtm", b % 2)])
            t.dma("sp", dst_rows[b * 128:b * 128 + rows, :], xb[0:rows, :], reads=[("xtm", b % 2)], sem=st)

    def mixer(l, tile):
        kind, ti = tile
        isP = kind == "P"
        NT = TT if isP else NS * LS
        nseq = 1 if isP else NS
        L = NT // nseq
        rmsnorm(xT, lambda c: ("xT", c), gN[:, l, 1, :], NT, pre=True)
        xl3 = lambda c: xlT[:, c, 0:nseq * (3 + L)].rearrange("p (s t) -> p s t", s=nseq)
        if isP:
            for c in range(2):
                t.op("dve", lambda c=c: V.tensor_copy(xlT[:, c, 0:3], convh_p[:, l, c, :]), reads=[("convh_p", l)], writes=[("xlT", c)])
            if ti > 0:
                t.op("dve", lambda: V.tensor_copy(kkT[:, :, 0:128], kk_halo[:, l, :, :]), reads=[("kk_halo", l)], writes=["kkT"])
                t.op("dve", lambda: V.tensor_copy(vtm[:, 0, :], v_halo[:, l, :]), reads=[("v_halo", l)], writes=["vtm"])
        else:
            for b in range(NS):
                for c in range(2):
                    ldc(xl3(c)[:, b, 0:3], I["sconv"][l, b][:, c * 128:(c + 1) * 128].rearrange("t p -> p t"), ("xlT", c), **NC_)
            ldc(lru_s[:, l, :, :], I["slru"][l].rearrange("b (c p) -> p b c", p=128), ("lru_s", l), **NC_)
            for b in range(NS):
                ldc(hs_s[0:64, l, b, :], I["sre"][l, b].rearrange("g p -> p g"), ("hs_s", l), **NC_)
                ldc(hs_s[64:128, l, b, :], I["sim"][l, b].rearrange("g p -> p g"), ("hs_s", l), **NC_)
        if stop == "m_pre":
            raise _Stop()
        def gen_proj():
            for c in range(12):
                if c > 0:
                    yield ("chunk", c)
                if stop is not None and stop.startswith("m_c") and c == int(stop[3:]):
                    raise _Stop()
                slot, sk = wnext("win")
                w_ = slot[:, 0:1024].rearrange("p (k f) -> p k f", k=8)
                po, pok = bank("pO")
                for k in range(8):
                    mm_(po[:, 0:NT], w_[:, k, :], hT[:, k, 0:NT], k == 0, k == 7, [sk, ("hT", k)], [pok], k == 7)
                if c < 2:
                    t.op("dve", lambda po=po, c=c: V.tensor_copy(uT[:, c, 0:NT], po[:, 0:NT]), reads=[pok], writes=[("uT", c)])
                    t.op("act", lambda c=c: A_.copy(ubf[:, c, 0:NT], uT[:, c, 0:NT]), reads=[("uT", c)], writes=[("ubf", c)])
                elif c < 6:
                    t.op("act", lambda po=po, c=c: A_.copy(qT[:, c - 2, 0:NT], po[:, 0:NT]), reads=[pok], writes=["qT"])
                elif c < 8:
                    t.op("act", lambda po=po, c=c: A_.copy(kkT[:, c - 6, 128:128 + NT], po[:, 0:NT]), reads=[pok], writes=["kkT"])
                elif c < 10:
                    cc = c - 8
                    t.op("act", lambda po=po, cc=cc: A_.copy(xl3(cc)[:, :, 3:3 + L], po[:, 0:NT].rearrange("p (s t) -> p s t", s=nseq)),
                         reads=[pok], writes=[("xlT", cc)])
                else:
                    cc = c - 10
                    t.op("act", lambda po=po, cc=cc: A_.activation(gg[:, cc, 0:NT], po[:, 0:NT], AF.Gelu_apprx_tanh), reads=[pok], writes=[("gg", cc)])
            yield "projdone"
            slot, sk = wnext("wvv")
            wv = slot[:, 0:2048].rearrange("p (k f) -> p k f", k=8)
            if isP:
                for b in range(NT // 128):
                    po, pok = bank("pO")
                    for k in range(8):
                        mm_(po[:, 0:256], hT[:, k, b * 128:(b + 1) * 128], wv[:, k, :], k == 0, k == 7, [sk, ("hT", k)], [pok], k == 7)
                    t.op("act", lambda po=po, b=b: A_.copy(vtm[:, 1 + b, :], po[:, 0:256]), reads=[pok], writes=["vtm"])
            else:
                for b in range(NS):
                    po, pok = bank("pO")
                    for k in range(8):
                        mm_(po[0:LS, 0:256], hT[:, k, b * LS:(b + 1) * LS], wv[:, k, :], k == 0, k == 7, [sk, ("hT", k)], [pok], k == 7)
                    t.op("act", lambda po=po, b=b: A_.copy(vnew[:, b, :], po[0:LS, 0:256]), reads=[pok], writes=["vnew"])
            if stop == "m_vv":
                raise _Stop()
            if tile_needs_kvx(tile):
                s1, sk1 = wnext("wkv")
                s2, sk2 = wnext("wxl")
                wkv = s1[:, 0:2048].rearrange("p (k f) -> p k f", k=8)
                wxl = s2[:, 0:2048].rearrange("p (k f) -> p k f", k=8)
                segs = [(NT - 128, 128, None)] if isP else [(b * LS, LS, b) for b in range(NS)]
                for (c0, n_, b) in segs:
                    po, pok = bank("pO")
                    for k in range(8):
                        mm_(po[0:n_, 0:256], hT[:, k, c0:c0 + n_], wkv[:, k, :], k == 0, k == 7, [sk1, ("hT", k)], [pok], k == 7)
                    for k in range(8):
                        mm_(po[0:n_, 256:512], hT[:, k, c0:c0 + n_], wxl[:, k, :], k == 0, k == 7, [sk2, ("hT", k)], [pok], k == 7)
                    t.op("dve", lambda po=po, n_=n_: V.tensor_copy(kvx[0:n_, :], po[0:n_, :]), reads=[pok], writes=["kvx"])
                    if isP:
                        t.dma("sp", O["kp"][l], kvx[:, 0:128], reads=["kvx"], sem=st)
                        t.dma("sp", O["vp"][l], kvx[:, 128:256], reads=["kvx"], sem=st)
                        t.dma("sp", O["convp"][l], kvx[125:128, 256:512], reads=["kvx"], sem=st)
                    else:
                        t.dma("sp", O["ks"][l, b], kvx[0:LS, 0:128], reads=["kvx"], sem=st)
                        t.dma("sp", O["vs"][l, b], kvx[0:LS, 128:256], reads=["kvx"], sem=st)
                        t.dma("sp", O["convs"][l, b], kvx[LS - 3:LS, 256:512], reads=["kvx"], sem=st)
            if stop == "m_proj":
                raise _Stop()
            yield "end"

        def gen_s5():
            segs = [(s * TS, TS, None) for s in range(NT // TS)] if isP else [(b * LS, LS, b) for b in range(NS)]
            n_ = TS if isP else LS
            fence(ACTK, ["s5z", "s5w_", "s5tmp"])
            fence(HK, ["s5A", "s5B"])
            fence([("xtm", 0), ("xtm", 1)], ["s5dec"])
            z3 = zbuf[:, 0:16 * n_].rearrange("p (g j) -> p g j", g=16)
            w3 = wbuf[:, 0:16 * n_].rearrange("p (g j) -> p g j", g=16)
            A3 = Abuf[:, 0:16 * n_].rearrange("p (g j) -> p g j", g=16)
            B3 = Bbuf[:, 0:16 * n_].rearrange("p (g j) -> p g j", g=16)
            d3 = decb[:, 0:16 * n_].rearrange("p (g j) -> p g j", g=16)
            t3 = tbuf[:, 0:8 * n_].rearrange("p (g j) -> p g j", g=8)
            t.op("dve", lambda: V.tensor_copy(d3, rdec[:, l, :].unsqueeze(2).to_broadcast([128, 16, n_])), reads=[("rdec", l)], writes=["s5dec"])
            t.op("dve", lambda: V.memset(d3[:, :, 0:1], 0.0), reads=[], writes=["s5dec"])
            if isP:
                t.op("dve", lambda: V.tensor_copy(hcur[:], hs_p[:, l, :]), reads=[("hs_p", l)], writes=["hcur"])
            XA = ["pA0", "pA1"]
            XB = ["pB0", "pB1"]
            fold32 = sb_once("fold32", [128, 16])
            nseg = len(segs)

            def Xmm(si, half):
                c0 = segs[si][0]
                for g8 in range(8):
                    g = half * 8 + g8
                    mm_(pX[:, g8 * 128:g8 * 128 + n_], s5w[:, 0, g, :], ubf[:, half, c0:c0 + n_], True, True, S5W + [("ubf", half)], XA, g8 == 7)
                for g8 in range(8):
                    g = half * 8 + g8
                    mm_(pX[:, 1024 + g8 * 128:1024 + g8 * 128 + n_], s5w[:, 1, g, :], ubf[:, half, c0:c0 + n_], True, True, S5W + [("ubf", half)], XB, g8 == 7)

            def Mod(si, half):
                x1 = pX[:, 0:1024].rearrange("p (g j) -> p g j", g=8)[:, :, 0:n_]
                x2 = pX[:, 1024:2048].rearrange("p (g j) -> p g j", g=8)[:, :, 0:n_]
                zh = z3[:, half * 8:half * 8 + 8, :]
                t.op("dve", lambda: V.tensor_tensor(zh, x1, s5t[:, 0, half * 8:half * 8 + 8, 0:n_], ALU.mult), reads=XA + S5R, writes=["s5z"])
                t.op("dve", lambda: V.tensor_tensor(t3, x2, s5t[:, 1, half * 8:half * 8 + 8, 0:n_], ALU.mult), reads=XB + S5R, writes=["s5tmp"])
                t.op("dve", lambda: V.tensor_tensor(zh, zh, t3, ALU.add), reads=["s5z", "s5tmp"], writes=["s5z"])

            def init_of(si):
                if not isP:
                    return hs_s[:, l, segs[si][2], :], ("hs_s", l)
                return hcur[:], "hcur"

            pys = {}
            Xmm(0, 0)
            Mod(0, 0)
            yield
            Xmm(0, 1)
            Mod(0, 1)
            yield
            for si, (c0, _n, b) in enumerate(segs):
                iap, ikey = init_of(si)
                t.op("dve", lambda: V.tensor_tensor(fold32[:], rdec[:, l, :], iap, ALU.mult), reads=[("rdec", l), ikey], writes=["fold32"])
                t.op("dve", lambda: V.tensor_tensor(z3[:, :, 0], z3[:, :, 0], fold32[:], ALU.add), reads=["s5z", "fold32"], writes=["s5z"])
                t.op("dve", lambda: V.tensor_tensor_scan(wbuf[:, 0:16 * n_], decb[:, 0:16 * n_], zbuf[:, 0:16 * n_], 0.0, ALU.mult, ALU.add),
                     reads=["s5z", "s5dec"], writes=["s5w_"])
                if si == 0:
                    fence(HK, ["s5A", "s5B"])
                t.op("dve", lambda: V.tensor_tensor(A3, w3, s5t[:, 0, :, 0:n_], ALU.mult), reads=["s5w_"] + S5R, writes=["s5A"])
                t.op("dve", lambda: V.tensor_tensor(B3, w3, s5t[:, 1, :, 0:n_], ALU.mult), reads=["s5w_"] + S5R, writes=["s5B"])
                t.op("dve", lambda: V.tensor_tensor(a32[:], w3[:, :, n_ - 1], s5t[:, 0, :, n_ - 1], ALU.mult), reads=["s5w_"] + S5R, writes=["a32"])
                t.op("dve", lambda: V.tensor_tensor(b32[:], w3[:, :, n_ - 1], s5t[:, 1, :, n_ - 1], ALU.mult), reads=["s5w_"] + S5R, writes=["b32"])
                yield
                if si + 1 < nseg:
                    Xmm(si + 1, 0)
                pyl = []
                for ch in range(2):
                    py, pyk = bank("pO")
                    pyl.append((py, pyk))
                    for g8 in range(8):
                        g = ch * 8 + g8
                        mm_(py[:, 0:n_], s5w[:, 2, g, :], A3[:, g, :], g8 == 0, False, S5W + ["s5A"], [pyk], False)
                        mm_(py[:, 0:n_], s5w[:, 3, g, :], B3[:, g, :], False, g8 == 7, S5W + ["s5B"], [pyk], g8 == 7)
                ph, phk = bank("pM")
                mm_(ph[:, 0:16], Pm[:], b32[:], True, True, ["Pm", "b32"], [phk], True)
                if si + 1 < nseg:
                    Mod(si + 1, 0)
                for ch in range(2):
                    py, pyk = pyl[ch]
                    t.op("dve", lambda py=py, ch=ch: V.scalar_tensor_tensor(ycat[:, ch, c0:c0 + n_], uT[:, ch, c0:c0 + n_], dD[:, l, ch:ch + 1],
                                                                             py[:, 0:n_], ALU.mult, ALU.add),
                         reads=[pyk, ("uT", ch), "dD"], writes=[("ycat", ch)])
                t.op("dve", lambda ph=ph: V.tensor_tensor(hcur[:], a32[:], ph[:, 0:16], ALU.add), reads=["a32", phk], writes=["hcur"])
                if not isP:
                    emit_state_out(hcur, "hcur", O["sres"][l, b], O["sims"][l, b])
                if si + 1 < nseg:
                    Xmm(si + 1, 1)
                    Mod(si + 1, 1)
                yield
            if isP:
                t.op("dve", lambda: V.tensor_copy(hs_p[:, l, :], hcur[:]), reads=["hcur"], writes=[("hs_p", l)])
                if ti == NPT - 1:
                    emit_state_out(hcur, "hcur", O["srep"][l], O["simp"][l])
            for ch in range(2):
                t.op("act", lambda ch=ch: A_.activation(ycat[:, ch, 0:NT], ycat[:, ch, 0:NT], AF.Gelu_apprx_tanh), reads=[("ycat", ch)], writes=[("ycat", ch)])
                t.op("dve", lambda ch=ch: V.tensor_copy(zb[:, ch, 0:NT], ycat[:, ch, 0:NT]), reads=[("ycat", ch)], writes=[("zb", ch)])
            for m in range(2):
                po, pok = bank("pO")
                for k in range(2):
                    mm_(po[:, 0:NT], wglu[:, l, k, m * 128:(m + 1) * 128], zb[:, k, 0:NT], k == 0, k == 1, ["wglu", ("zb", k)], [pok], k == 1)
                t.op("act", lambda po=po: A_.activation(sig[:, 0:NT], po[:, 0:NT], AF.Sigmoid), reads=[pok], writes=["sig"])
                t.op("dve", lambda m=m: V.tensor_tensor(ycat[:, m, 0:NT], ycat[:, m, 0:NT], sig[:, 0:NT], ALU.mult), reads=["sig", ("ycat", m)], writes=[("ycat", m)])
            yield

        def gen_attn():
            if not isP:
                cur_b = [None]
            nq_chunks = NT // 64 if isP else NS
            for qc in range(nq_chunks):
                if isP:
                    q0, nq = 64 * qc, 64
                    blocks = []
                    c = qc
                    if c % 2 == 0:
                        fb, hb, hrows = c // 2, c // 2 + 1, slice(0, 64)
                    else:
                        hb, hrows, fb = (c - 1) // 2, slice(64, 128), (c + 1) // 2
                    if ti == 0 and c == 0:
                        blocks = [(hb, hrows)]
                    elif ti == 0 and c == 1:
                        blocks = [(fb, slice(0, 128))]
                    else:
                        blocks = [(fb, slice(0, 128)), (hb, hrows)]
                else:
                    b = qc
                    q0, nq = LS * b, LS
                    ldc(ckf[:], I["ck"][l, b], "ckf")
                    ldc(cvf[:], I["cv"][l, b], "cvf")
                    for h in range(2):
                        for d2 in range(2):
                            t.op("dve", lambda h=h, d2=d2: V.tensor_copy(ckd[:, h, d2 * 64:(d2 + 1) * 64], ckf[:, h * 64:(h + 1) * 64]), reads=["ckf"], writes=["ckd"])
                            t.op("act", lambda h=h, d2=d2: A_.copy(vtm[:, 0, (2 * h + d2) * 64:(2 * h + d2 + 1) * 64], cvf[:, h * 64:(h + 1) * 64]), reads=["cvf"], writes=["vtm"])
                    for h in range(2):
                        ptk, ptkk = bank("pM")
                        t.op("pe", lambda ptk=ptk, h=h: PE_.transpose(ptk[:, 0:128], ckd[:, h, :], idn[:]), reads=["ckd", "idn"], writes=[ptkk])
                        t.op("dve", lambda ptk=ptk, h=h: V.tensor_copy(ckT[:, h, :], ptk[:, 0:128]), reads=[ptkk], writes=["ckT"])
                    blocks = [("cache", slice(0, 128)), ("new", slice(0, LS))]
                    if stop == "m_a1":
                        raise _Stop()
                nqc = 2 * nq
                for h in range(2):
                    pscs = [(PS["pM0"], "pM0"), (PS["pM1"], "pM1")]
                    pb_ = pT[h % 2]
                    pbk = ("pT", h % 2)
                    for bi, (blk, rows) in enumerate(blocks):
                        for gp in range(2):
                            psc, psk = pscs[gp]
                            r_ = slice(gp * 64, gp * 64 + 64)
                            if blk == "cache":
                                lk, lkk, M_ = ckT[r_, h, :], "ckT", 128
                            elif blk == "new":
                                lk, lkk, M_ = kkT[r_, h, 128 + q0:128 + q0 + LS], "kkT", LS
                            else:
                                lk, lkk, M_ = kkT[r_, h, blk * 128:(blk + 1) * 128], "kkT", 128
                            for i_ in range(2):
                                mm_(psc[0:M_, bi * 128 + i_ * nq: bi * 128 + (i_ + 1) * nq], lk, qT[r_, 2 * h + i_, q0:q0 + nq], True, True,
                                    [lkk, "qT"], [psk], (bi == len(blocks) - 1 and i_ == 1))
                    if stop == "m_a2":
                        raise _Stop()
                    for bi, (blk, rows) in enumerate(blocks):
                        for gp in range(2):
                            psc, psk = pscs[gp]
                            t.op("act", lambda psc=psc, pb_=pb_, bi=bi, rows=rows, gp=gp: A_.activation(
                                pb_[rows, bi, gp * nqc:(gp + 1) * nqc], psc[rows, bi * 128:bi * 128 + nqc], AF.Exp, scale=0.125),
                                reads=[psk], writes=[pbk])
                    if stop == "m_a3":
                        raise _Stop()
                    po, pok = bank("pO")
                    for bi, (blk, rows) in enumerate(blocks):
                        if blk == "cache":
                            lv, lvk = vtm[rows, 0, h * 128:(h + 1) * 128], "vtm"
                        elif blk == "new":
                            lv, lvk = vnew[rows, b, h * 128:(h + 1) * 128], "vnew"
                        else:
                            lv, lvk = vtm[rows, blk, h * 128:(h + 1) * 128], "vtm"
                        mm_(po[:, 0:2 * nqc], lv, pb_[rows, bi, 0:2 * nqc], bi == 0, bi == len(blocks) - 1, [lvk, pbk], [pok], bi == len(blocks) - 1)
                    pd_, pdk = bank("pB")
                    for bi, (blk, rows) in enumerate(blocks):
                        mm_(pd_[:, 0:2 * nqc], ones_b[rows, :], pb_[rows, bi, 0:2 * nqc], bi == 0, False, ["ones_b", pbk], [pdk], False)
                    esr = esbP[0:1, l, h, :] if isP else esbS[0:1, l, h, :]
                    mm_(pd_[:, 0:2 * nqc], ones_b[0:1, :], esr, False, True, ["ones_b", "esb"], [pdk], True)
                    if stop == "m_a4":
                        raise _Stop()
                    dsb = dn_sb[h % 2]
                    dk = ("dnsb", h % 2)
                    t.op("act", lambda pd_=pd_, dsb=dsb: A_.activation(dsb[:, 0:2 * nqc], pd_[:, 0:2 * nqc], AF.Ln), reads=[pdk], writes=[dk])
                    t.op("act", lambda dsb=dsb: A_.activation(dsb[:, 0:2 * nqc], dsb[:, 0:2 * nqc], AF.Exp, scale=-1.0), reads=[dk], writes=[dk])
                    for gp in range(2):
                        r_ = slice(gp * 64, gp * 64 + 64)
                        cs = slice(gp * nqc, (gp + 1) * nqc)
                        t.op("dve", lambda po=po, r_=r_, dsb=dsb, cs=cs, h=h: V.tensor_tensor(
                            ycat[r_, 2 + 2 * h:4 + 2 * h, q0:q0 + nq], po[r_, cs].rearrange("p (i t) -> p i t", i=2),
                            dsb[r_, cs].rearrange("p (i t) -> p i t", i=2), ALU.mult),
                            reads=[pok, dk], writes=[("ycat", 2 + 2 * h), ("ycat", 3 + 2 * h)])
                    yield
            if isP:
                t.op("dve", lambda: V.tensor_copy(kk_halo[:, l, :, :], kkT[:, :, NT:NT + 128]), reads=["kkT"], writes=[("kk_halo", l)])
                t.op("dve", lambda: V.tensor_copy(v_halo[:, l, :], vtm[:, NT // 128, :]), reads=["vtm"], writes=[("v_halo", l)])
            yield

        def gen_lru():
            for c in range(2):
                x3 = xl3(c)
                xc3 = xc[:, 0:NT].rearrange("p (s t) -> p s t", s=nseq)
                t.op("dve", lambda x3=x3, c=c: V.tensor_scalar(xc3, x3[:, :, 0:L], cw[:, l, 0, c:c + 1], cb[:, l, c:c + 1], ALU.mult, ALU.add),
                     reads=[("xlT", c), "cw", "cb"], writes=["xc"])
                for tt_ in range(1, 4):
                    t.op("dve", lambda x3=x3, c=c, tt_=tt_: V.scalar_tensor_tensor(xc3, x3[:, :, tt_:tt_ + L], cw[:, l, tt_, c:c + 1], xc3, ALU.mult, ALU.add),
                         reads=[("xlT", c), "cw", "xc"], writes=["xc"])
                if isP:
                    t.op("dve", lambda c=c: V.tensor_copy(convh_p[:, l, c, :], xlT[:, c, L:L + 3]), reads=[("xlT", c)], writes=[("convh_p", l)])
                t.op("act", lambda: A_.copy(xcb[:, 0:NT], xc[:, 0:NT]), reads=["xc"], writes=["xcb"])
                yield
                pa, pak = bank("pA")
                pb, pbk2 = bank("pB")
                mm_(pa[:, 0:NT], wlru[:, l, c, :], xcb[:, 0:NT], True, True, ["wlru", "xcb"], [pak], True)
                mm_(pb[:, 0:NT], wlru[:, l, 2 + c, :], xcb[:, 0:NT], True, True, ["wlru", "xcb"], [pbk2], True)
                t.op("act", lambda pa=pa, c=c: A_.activation(rr[:, 0:NT], pa[:, 0:NT], AF.Sigmoid, bias=ba[:, l, c:c + 1], scale=1.0), reads=[pak, "ba"], writes=["rr"])
                t.op("act", lambda pb=pb, c=c: A_.activation(ii[:, 0:NT], pb[:, 0:NT], AF.Sigmoid, bias=bx[:, l, c:c + 1], scale=1.0), reads=[pbk2, "bx"], writes=["ii"])
                t.op("act", lambda c=c: A_.activation(aa[:, 0:NT], rr[:, 0:NT], AF.Exp, scale=c1[:, l, c:c + 1]), reads=["rr", "c1"], writes=["aa"])
                t.op("act", lambda c=c: A_.activation(mm[:, 0:NT], rr[:, 0:NT], AF.Exp, scale=c2[:, l, c:c + 1]), reads=["rr", "c2"], writes=["mm"])
                t.op("act", lambda: A_.activation(mm[:, 0:NT], mm[:, 0:NT], AF.Sqrt, bias=oneb[:, 0:1], scale=-1.0), reads=["mm", "oneb"], writes=["mm"])
                yield
                t.op("dve", lambda: V.tensor_tensor(mm[:, 0:NT], mm[:, 0:NT], ii[:, 0:NT], ALU.mult), reads=["mm", "ii"], writes=["mm"])
                t.op("dve", lambda: V.tensor_tensor(mm[:, 0:NT], mm[:, 0:NT], xc[:, 0:NT], ALU.mult), reads=["mm", "xc"], writes=["mm"])
                for s_ in range(nseq):
                    cs = slice(s_ * L, (s_ + 1) * L)
                    init = lru_p[:, l, c:c + 1] if isP else lru_s[:, l, s_, c:c + 1]
                    ik = ("lru_p", l) if isP else ("lru_s", l)
                    t.op("dve", lambda cs=cs, init=init, c=c: V.tensor_tensor_scan(ycat[:, 6 + c, cs], aa[:, cs], mm[:, cs], init, ALU.mult, ALU.add),
                         reads=["aa", "mm", ik], writes=[("ycat", 6 + c)])
                    if isP:
                        t.op("dve", lambda c=c: V.tensor_copy(lru_p[:, l, c:c + 1], ycat[:, 6 + c, L - 1:L]), reads=[("ycat", 6 + c)], writes=[("lru_p", l)])
                    else:
                        t.op("dve", lambda c=c, s_=s_: V.tensor_copy(lru_s[:, l, s_, c:c + 1], ycat[:, 6 + c, (s_ + 1) * L - 1:(s_ + 1) * L]),
                             reads=[("ycat", 6 + c)], writes=[("lru_s", l)])
                t.op("dve", lambda c=c: V.tensor_tensor(ycat[:, 6 + c, 0:NT], ycat[:, 6 + c, 0:NT], gg[:, c, 0:NT], ALU.mult),
                     reads=[("ycat", 6 + c), ("gg", c)], writes=[("ycat", 6 + c)])
            if isP and ti == NPT - 1:
                t.dma("sp", O["lrup"][l].rearrange("(c p) -> p c", p=128), lru_p[:, l, :], reads=[("lru_p", l)], sem=st, **NC_)
            if not isP:
                t.dma("sp", O["lrus"][l].rearrange("b (c p) -> p b c", p=128), lru_s[:, l, :, :], reads=[("lru_s", l)], sem=st, **NC_)
            yield

        g_proj = gen_proj()
        g_s5 = gen_s5()

        def adv_proj(until):
            while True:
                try:
                    r_ = next(g_proj)
                except StopIteration:
                    return
                if r_ == until:
                    return
        adv_proj(("chunk", 2))
        next(g_s5)
        adv_proj(("chunk", 6))
        next(g_s5)
        adv_proj(None)
        gens = [g_s5, gen_attn(), gen_lru()]
        while gens:
            for g_ in list(gens):
                try:
                    next(g_)
                except StopIteration:
                    gens.remove(g_)

        fence(["s5A", "s5B"], HK)
        rmsnorm(ycat, lambda c: ("ycat", c), gN[:, l, 2, :], NT, ngrp=[([0, 1], 2), ([2, 3, 4, 5], 1), ([6, 7], 2)])
        for m in range(8):
            slot, sk = wnext("wout")
            w_ = slot[:, 0:1024].rearrange("p (k f) -> p k f", k=8)
            po, pok = bank("pO")
            for k in range(8):
                mm_(po[:, 0:NT], w_[:, k, :], hT[:, k, 0:NT], k == 0, k == 7, [sk, ("hT", k)], [pok], k == 7)
            t.op("dve", lambda po=po, m=m: V.tensor_tensor(xT[:, m, 0:NT], xT[:, m, 0:NT], po[:, 0:NT], ALU.add), reads=[pok, ("xT", m)], writes=[("xT", m)])
            stat_acc(m, NT)

    oneb = sb("oneb", [128, 1])
    t.op("dve", lambda: V.memset(oneb[:], 1.0), writes=["oneb"])

    def emit_state_out(src, srckey, dre, dim):
        t.dma("sp", dre.rearrange("g p -> p g"), src[0:64, :], reads=[srckey], sem=st, **NC_)
        t.dma("sp", dim.rearrange("g p -> p g"), src[64:128, :], reads=[srckey], sem=st, **NC_)

    for tile in tiles:
        kind, ti = tile
        if kind == "P":
            NT = TT
            load_x(I["xp"][ti * TT:(ti + 1) * TT, :], NT)
        else:
            NT = NS * LS
            load_x(I["xs"], NT)
        if stop == "load_x":
            dump(xT[:, :, 0:64], XK, 512)
            return finish()
        for l in range(2):
            load_s5(l)
            ffn(l, 1, NT, pre=(l > 0))
            if stop == "ffn1":
                dump(xT[:, :, 0:64], XK, 512)
                return finish()
            try:
                mixer(l, tile)
            except _Stop:
                dump(xT[:, :, 0:64], XK, 512)
                dump(ycat[:, :, 0:64], [("ycat", c) for c in range(8)], 512)
                dump(uT[:, :, 0:64], [("uT", 0), ("uT", 1)], 128)
                dump(gg[:, :, 0:64], [("gg", 0), ("gg", 1)], 128)
                dump(xlT[:, :, 0:76], [("xlT", 0), ("xlT", 1)], 152)
                return finish()
            if stop == "mixer":
                dump(xT[:, :, 0:64], XK, 512)
                dump(ycat[:, :, 0:64], [("ycat", c) for c in range(8)], 512)
                return finish()
            ffn(l, 2, NT, pre=True)
        if kind == "P":
            store_y(O["yp"][ti * TT:(ti + 1) * TT, :], NT)
        else:
            store_y(O["ys"], NT)
    return finish()


_CACHE = {}
PCORES = [0, 1, 4, 5]


def kernel(**inp):
    f = lambda a: np.ascontiguousarray(np.asarray(a, dtype=np.float32))
    if "nc" not in _CACHE:
        _CACHE["nc"] = build_program()
    nc = _CACHE["nc"]
    W = {n: f(inp[n]) for n in WNAMES}
    xp, xs = f(inp["x_prompt"]), f(inp["x_sample"])
    ck, cv = f(inp["cache_k"]), f(inp["cache_v"])
    sre, sim = f(inp["state_ssm_re"]), f(inp["state_ssm_im"])
    sconv, slru = f(inp["state_conv"]), f(inp["state_lru"])
    in_maps = []
    zero_x = np.zeros((SEQ, D), np.float32)
    for c in range(8):
        sl = slice(NS * c, NS * c + NS)
        m = {"xp": (xp[PCORES.index(c)] if c in PCORES else zero_x), "xs": np.ascontiguousarray(xs[sl].reshape(NS * LS, D)),
             "ck": np.ascontiguousarray(ck[:, sl].reshape(2, NS, 128, 128)),
             "cv": np.ascontiguousarray(cv[:, sl].reshape(2, NS, 128, 128)),
             "sre": np.ascontiguousarray(sre[:, sl]), "sim": np.ascontiguousarray(sim[:, sl]),
             "sconv": np.ascontiguousarray(sconv[:, sl]), "slru": np.ascontiguousarray(slru[:, sl])}
        m.update(W)
        in_maps.append(m)
    res = run_bass_kernel_spmd(nc, in_maps, core_ids=list(range(8))).results
    cat = lambda name, ax: np.concatenate([res[c][name] for c in range(8)], axis=ax)
    stack4 = lambda name: np.stack([res[c][name] for c in PCORES], axis=1)
    y_prompt = np.stack([res[c]["yp"] for c in PCORES], axis=0)
    y_sample = cat("ys", 0).reshape(32, LS, D)
    k_p = stack4("kp").reshape(2, 4, 128, 2, 64)
    v_p = stack4("vp").reshape(2, 4, 128, 2, 64)
    sre_p, sim_p = stack4("srep"), stack4("simp")
    conv_p, lru_p = stack4("convp"), stack4("lrup")
    k_s = cat("ks", 1).reshape(2, 32, LS, 2, 64)
    v_s = cat("vs", 1).reshape(2, 32, LS, 2, 64)
    sre_s, sim_s = cat("sres", 1), cat("sims", 1)
    conv_s, lru_s = cat("convs", 1), cat("lrus", 1)
    outs = (y_prompt, y_sample, k_p, v_p, sre_p, sim_p, conv_p, lru_p, k_s, v_s, sre_s, sim_s, conv_s, lru_s)
    return tuple(np.ascontiguousarray(o, dtype=np.float32) for o in outs)
```
